# Optimizing a Trainium2 kernel written in Bass

```python
import jax
import jax.numpy as jnp
from jax import lax
import numpy as np

D_MODEL = 2048
BATCH = 8
SEQ = 4096
DEPTH = 4

CTX_LEN = 256
GRID_W = 64
CONV_CH = 512
CONV_WIDTH = 31
MLA_HEADS = 8
MLA_Q_RANK = 512
MLA_KV_RANK = 256
MLA_NOPE = 128
MLA_ROPE = 64
MLA_V = 128
GQA_HEADS = 8
GQA_KV_HEADS = 2
GQA_HEAD_DIM = 64
WINDOW = 128
BLOCK = 128
N_BRANCH = 3
D_FF = -(-8 * D_MODEL // (3 * 256)) * 256
ROPE_DIM = 64
ROPE_BASE = 10000.0
EPS = 1e-6
NEG_INF = -1e30

COL_MLA_KV = 0
COL_MLA_KR = COL_MLA_KV + MLA_KV_RANK
COL_GQA_K = COL_MLA_KR + MLA_ROPE
COL_GQA_V = COL_GQA_K + GQA_KV_HEADS * GQA_HEAD_DIM
KV_COLS = COL_GQA_V + GQA_KV_HEADS * GQA_HEAD_DIM
COL_MLA_Q = KV_COLS
COL_GQA_Q = COL_MLA_Q + MLA_Q_RANK
COL_CONV = COL_GQA_Q + GQA_HEADS * GQA_HEAD_DIM
COL_GATE = COL_CONV + 2 * CONV_CH
IN_COLS = COL_GATE + N_BRANCH * D_MODEL

kernel_name = 'hybrid_conv_mla_swa_dit_block'


def _rmsnorm(x, g):
    xf = x.astype(jnp.float32)
    y = xf * lax.rsqrt(jnp.mean(xf * xf, axis=-1, keepdims=True) + EPS)
    return (y * g.astype(jnp.float32)).astype(x.dtype)


def _layernorm(x, g, b):
    xf = x.astype(jnp.float32)
    xc = xf - jnp.mean(xf, axis=-1, keepdims=True)
    y = xc * lax.rsqrt(jnp.mean(xc * xc, axis=-1, keepdims=True) + EPS)
    return (y * g.astype(jnp.float32) + b.astype(jnp.float32)).astype(x.dtype)


def _modulate(h, shift, scale):
    return h * (1 + scale) + shift


def _axial_rope_tables(length):
    rows = length // GRID_W
    row = jnp.repeat(jnp.arange(rows), GRID_W).astype(jnp.float32)
    col = jnp.tile(jnp.arange(GRID_W), rows).astype(jnp.float32)
    n_freq = ROPE_DIM // 4
    inv_freq = ROPE_BASE ** (-jnp.arange(n_freq, dtype=jnp.float32) / n_freq)
    a_row = row[:, None] * inv_freq[None, :]
    a_col = col[:, None] * inv_freq[None, :]
    ang = jnp.concatenate([a_row, a_row, a_col, a_col], axis=-1)
    return jnp.cos(ang), jnp.sin(ang)


def _apply_axial_rope(x, rope):
    cos, sin = rope
    xs = x.reshape(x.shape[:-1] + (2, 2, ROPE_DIM // 4))
    rot = jnp.concatenate([-xs[..., 1:, :], xs[..., :1, :]], axis=-2).reshape(x.shape)
    return x * cos[None, :, None, :].astype(x.dtype) + rot * sin[None, :, None, :].astype(x.dtype)


def _attend(qb, keys, values, masks, sink):
    scale = qb.shape[-1] ** -0.5
    scores = []
    for k, m in zip(keys, masks):
        s = jnp.einsum('bqgrd,bkgd->bgrqk', qb, k).astype(jnp.float32) * scale
        if m is not None:
            s = jnp.where(m, s, NEG_INF)
        scores.append(s)
    if sink is not None:
        scores.append(jnp.broadcast_to(sink.astype(jnp.float32)[None, :, :, None, None], scores[0].shape[:-1] + (1,)))
    p = jax.nn.softmax(jnp.concatenate(scores, axis=-1), axis=-1)
    out = None
    off = 0
    for v in values:
        n = v.shape[1]
        o = jnp.einsum('bgrqk,bkgd->bqgrd', p[..., off:off + n].astype(v.dtype), v)
        out = o if out is None else out + o
        off += n
    return out


def _latent_attention(q, k, v, k_ctx, v_ctx, sink, window):
    B, L, H, dk = q.shape
    G = k.shape[2]
    R = H // G
    dv = v.shape[-1]
    n_blk = L // BLOCK
    q_blocks = jnp.moveaxis(q.reshape(B, n_blk, BLOCK, G, R, dk), 1, 0)
    sink_gr = None if sink is None else sink.reshape(G, R)
    if window is not None:
        pad = ((0, 0), (window, window), (0, 0), (0, 0))
        k_pad = jnp.pad(k, pad)
        v_pad = jnp.pad(v, pad)
        span = BLOCK + 2 * window

    def one_block(args):
        qb, blk = args
        start = blk * BLOCK
        if window is None:
            kb, vb, mask = k, v, None
        else:
            kb = lax.dynamic_slice_in_dim(k_pad, start, span, axis=1)
            vb = lax.dynamic_slice_in_dim(v_pad, start, span, axis=1)
            q_pos = start + jnp.arange(BLOCK)
            k_pos = start - window + jnp.arange(span)
            mask = ((jnp.abs(q_pos[:, None] - k_pos[None, :]) <= window)
                    & (k_pos >= 0)[None, :] & (k_pos < L)[None, :])
        return _attend(qb, [kb, k_ctx], [vb, v_ctx], [mask, None], sink_gr)

    out = lax.map(one_block, (q_blocks, jnp.arange(n_blk)))
    return jnp.moveaxis(out, 0, 1).reshape(B, L, H, dv)


def _context_attention(q, k, v, sink):
    B, C, H, dk = q.shape
    G = k.shape[2]
    sink_gr = None if sink is None else sink.reshape(G, H // G)
    o = _attend(q.reshape(B, C, G, H // G, dk), [k], [v], [None], sink_gr)
    return o.reshape(B, C, H, v.shape[-1])


def _mla_q(p, g_q_a, w_q_up, rope):
    B, L = p.shape[:2]
    q = (_rmsnorm(p[..., COL_MLA_Q:COL_MLA_Q + MLA_Q_RANK], g_q_a) @ w_q_up).reshape(B, L, MLA_HEADS, MLA_NOPE + MLA_ROPE)
    q_nope, q_rope = q[..., :MLA_NOPE], q[..., MLA_NOPE:]
    if rope is not None:
        q_rope = _apply_axial_rope(q_rope, rope)
    return jnp.concatenate([q_nope, q_rope], axis=-1)


def _mla_kv(p, g_kv_a, w_kv_up, rope):
    B, L = p.shape[:2]
    c_kv = p[..., COL_MLA_KV:COL_MLA_KV + MLA_KV_RANK]
    k_rope = p[..., COL_MLA_KR:COL_MLA_KR + MLA_ROPE][:, :, None, :]
    if rope is not None:
        k_rope = _apply_axial_rope(k_rope, rope)
    kv = (_rmsnorm(c_kv, g_kv_a) @ w_kv_up).reshape(B, L, MLA_HEADS, MLA_NOPE + MLA_V)
    k = jnp.concatenate([kv[..., :MLA_NOPE], jnp.broadcast_to(k_rope, (B, L, MLA_HEADS, MLA_ROPE))], axis=-1)
    return k, kv[..., MLA_NOPE:]


def _gqa_q(p, rope):
    B, L = p.shape[:2]
    q = p[..., COL_GQA_Q:COL_GQA_Q + GQA_HEADS * GQA_HEAD_DIM].reshape(B, L, GQA_HEADS, GQA_HEAD_DIM)
    return q if rope is None else _apply_axial_rope(q, rope)


def _gqa_kv(p, rope):
    B, L = p.shape[:2]
    k = p[..., COL_GQA_K:COL_GQA_V].reshape(B, L, GQA_KV_HEADS, GQA_HEAD_DIM)
    v = p[..., COL_GQA_V:KV_COLS].reshape(B, L, GQA_KV_HEADS, GQA_HEAD_DIM)
    if rope is not None:
        k = _apply_axial_rope(k, rope)
    return k, v


def _conv_branch(p, conv_w, conv_b, ln_g, ln_b, w_conv_out):
    a = p[..., COL_CONV:COL_CONV + CONV_CH]
    b = p[..., COL_CONV + CONV_CH:COL_GATE]
    u = a * jax.nn.sigmoid(b)
    u = lax.conv_general_dilated(u, conv_w[:, None, :], window_strides=(1,),
                                 padding=[(CONV_WIDTH // 2, CONV_WIDTH // 2)],
                                 dimension_numbers=('NWC', 'WIO', 'NWC'),
                                 feature_group_count=CONV_CH) + conv_b
    u = jax.nn.silu(_layernorm(u, ln_g, ln_b))
    return u @ w_conv_out


def _merge(p, y_conv, y_mla, y_gqa, w_out):
    g = jax.nn.sigmoid(p[..., COL_GATE:IN_COLS]).reshape(p.shape[:-1] + (N_BRANCH, D_MODEL))
    y = g[..., 0, :] * y_conv + g[..., 1, :] * y_mla + g[..., 2, :] * y_gqa
    return y @ w_out


def _swiglu(h, w_in, w_out):
    u = h @ w_in
    return (jax.nn.silu(u[..., :D_FF]) * u[..., D_FF:]) @ w_out


def _mixing(p, ctx_kv, rope, lp):
    mk_c, mv_c, gk_c, gv_c = ctx_kv
    B, L = p.shape[:2]
    mq = _mla_q(p, lp['g_q_a'], lp['w_q_up'], rope)
    gq = _gqa_q(p, rope)
    if rope is None:
        o_mla = _context_attention(mq, mk_c, mv_c, None)
        o_gqa = _context_attention(gq, gk_c, gv_c, lp['gqa_sink'])
    else:
        mk, mv = _mla_kv(p, lp['g_kv_a'], lp['w_kv_up'], rope)
        gk, gv = _gqa_kv(p, rope)
        o_mla = _latent_attention(mq, mk, mv, mk_c, mv_c, None, None)
        o_gqa = _latent_attention(gq, gk, gv, gk_c, gv_c, lp['gqa_sink'], WINDOW)
    y_mla = o_mla.reshape(B, L, MLA_HEADS * MLA_V) @ lp['w_mla_out']
    y_gqa = o_gqa.reshape(B, L, GQA_HEADS * GQA_HEAD_DIM) @ lp['w_gqa_out']
    y_conv = _conv_branch(p, lp['conv_w'], lp['conv_b'], lp['conv_ln_g'], lp['conv_ln_b'], lp['w_conv_out'])
    return _merge(p, y_conv, y_mla, y_gqa, lp['w_out'])


def setup_inputs(seed: int = 0) -> dict:
    key = jax.random.key(seed)
    ks = jax.random.split(key, 25)

    def nrm(k, shape, scale):
        return jax.random.normal(k, shape, jnp.float32) * scale

    def gain(k, shape):
        return 1.0 + nrm(k, shape, 0.05)

    return {
        'x': nrm(ks[0], (BATCH, SEQ, D_MODEL), 1.0),
        'c': nrm(ks[1], (BATCH, D_MODEL), 1.0),
        'ctx': nrm(ks[2], (BATCH, CTX_LEN, D_MODEL), 1.0),
        'c_ctx': nrm(ks[3], (D_MODEL,), 1.0),
        'w_ada': nrm(ks[4], (DEPTH, D_MODEL, 6 * D_MODEL), 0.5 * D_MODEL ** -0.5),
        'b_ada': nrm(ks[5], (DEPTH, 6 * D_MODEL), 0.01),
        'g_mix': gain(ks[6], (DEPTH, D_MODEL)),
        'w_in': nrm(ks[7], (DEPTH, D_MODEL, IN_COLS), D_MODEL ** -0.5),
        'conv_w': nrm(ks[8], (DEPTH, CONV_WIDTH, CONV_CH), CONV_WIDTH ** -0.5),
        'conv_b': nrm(ks[9], (DEPTH, CONV_CH), 0.01),
        'conv_ln_g': gain(ks[10], (DEPTH, CONV_CH)),
        'conv_ln_b': nrm(ks[11], (DEPTH, CONV_CH), 0.01),
        'w_conv_out': nrm(ks[12], (DEPTH, CONV_CH, D_MODEL), CONV_CH ** -0.5),
        'g_q_a': gain(ks[13], (DEPTH, MLA_Q_RANK)),
        'w_q_up': nrm(ks[14], (DEPTH, MLA_Q_RANK, MLA_HEADS * (MLA_NOPE + MLA_ROPE)), MLA_Q_RANK ** -0.5),
        'g_kv_a': gain(ks[15], (DEPTH, MLA_KV_RANK)),
        'w_kv_up': nrm(ks[16], (DEPTH, MLA_KV_RANK, MLA_HEADS * (MLA_NOPE + MLA_V)), MLA_KV_RANK ** -0.5),
        'w_mla_out': nrm(ks[17], (DEPTH, MLA_HEADS * MLA_V, D_MODEL), (MLA_HEADS * MLA_V) ** -0.5),
        'gqa_sink': nrm(ks[18], (DEPTH, GQA_HEADS), 0.5),
        'w_gqa_out': nrm(ks[19], (DEPTH, GQA_HEADS * GQA_HEAD_DIM, D_MODEL), (GQA_HEADS * GQA_HEAD_DIM) ** -0.5),
        'w_out': nrm(ks[20], (DEPTH, D_MODEL, D_MODEL), D_MODEL ** -0.5),
        'g_ffn': gain(ks[21], (DEPTH, D_MODEL)),
        'w_ffn_in': nrm(ks[22], (DEPTH, D_MODEL, 2 * D_FF), D_MODEL ** -0.5),
        'w_ffn_out': nrm(ks[23], (DEPTH, D_FF, D_MODEL), D_FF ** -0.5),
        'g_final': gain(ks[24], (D_MODEL,)),
    }


def reference(x, c, ctx, c_ctx, w_ada, b_ada, g_mix, w_in, conv_w, conv_b, conv_ln_g, conv_ln_b,
              w_conv_out, g_q_a, w_q_up, g_kv_a, w_kv_up, w_mla_out, gqa_sink, w_gqa_out, w_out,
              g_ffn, w_ffn_in, w_ffn_out, g_final):
    L = x.shape[1]
    rope = _axial_rope_tables(L)
    xc = ctx
    for l in range(DEPTH):
        last = l == DEPTH - 1
        lp = {'g_q_a': g_q_a[l], 'w_q_up': w_q_up[l], 'g_kv_a': g_kv_a[l], 'w_kv_up': w_kv_up[l],
              'w_mla_out': w_mla_out[l], 'gqa_sink': gqa_sink[l], 'w_gqa_out': w_gqa_out[l],
              'conv_w': conv_w[l], 'conv_b': conv_b[l], 'conv_ln_g': conv_ln_g[l], 'conv_ln_b': conv_ln_b[l],
              'w_conv_out': w_conv_out[l], 'w_out': w_out[l]}
        mod = (jax.nn.silu(c) @ w_ada[l] + b_ada[l])[:, None, :]
        mod_c = (jax.nn.silu(c_ctx) @ w_ada[l] + b_ada[l])[None, None, :]
        sh1, sc1, gt1, sh2, sc2, gt2 = jnp.split(mod, 6, axis=-1)
        csh1, csc1, cgt1, csh2, csc2, cgt2 = jnp.split(mod_c, 6, axis=-1)
        w_in_l = w_in[l]
        p = _modulate(_rmsnorm(x, g_mix[l]), sh1, sc1) @ w_in_l
        hc = _modulate(_rmsnorm(xc, g_mix[l]), csh1, csc1)
        pc = hc @ (w_in_l[:, :KV_COLS] if last else w_in_l)
        ctx_kv = _mla_kv(pc, g_kv_a[l], w_kv_up[l], None) + _gqa_kv(pc, None)
        x = x + gt1 * _mixing(p, ctx_kv, rope, lp)
        if not last:
            xc = xc + cgt1 * _mixing(pc, ctx_kv, None, lp)
        x = x + gt2 * _swiglu(_modulate(_rmsnorm(x, g_ffn[l]), sh2, sc2), w_ffn_in[l], w_ffn_out[l])
        if not last:
            xc = xc + cgt2 * _swiglu(_modulate(_rmsnorm(xc, g_ffn[l]), csh2, csc2), w_ffn_in[l], w_ffn_out[l])
    return _rmsnorm(x, g_final)
```

```python
import numpy as np
import os
from contextlib import ExitStack
import concourse.bass as bass
import concourse.mybir as mybir
from concourse.bass_utils import run_bass_kernel_spmd

F32 = mybir.dt.float32
BF16 = mybir.dt.bfloat16
AF = mybir.ActivationFunctionType
ALU = mybir.AluOpType

D = 2048
NK = 16
CTX = 256
DFF = 5632
NFF = 44
INC = 8768
EPS = 1e-6
ENGS = ["pe", "act", "dve", "pool", "sp"]
SAME_ENGINE_SYNC = True

S_GMIX, S_GFFN, S_BADA, S_GQA, S_GKV, S_CB, S_LNG, S_LNB, S_CW, S_SINK, S_GFIN = 0, 16, 32, 128, 132, 134, 138, 142, 146, 270, 278
NS = 294


class Buf:
    __slots__ = ("name", "w", "r", "dsem", "dcnt")

    def __init__(self, name):
        self.name = name
        self.w = None
        self.r = {}
        self.dsem = None
        self.dcnt = 0


class Tile:
    def __init__(self, t, name):
        self.t = t
        self.b = Buf(name)

    def __getitem__(self, k):
        return self.t[k]


class Phase:
    def __init__(self, nc, name, persistent=()):
        self.nc = nc
        self.name = name
        self.es = ExitStack()
        self.ops = {e: [] for e in ENGS}
        self.sem = {e: nc.alloc_semaphore(name=f"{name}_{e}") for e in ENGS}
        self.allsems = list(self.sem.values())
        self.cnt = {e: 0 for e in ENGS}
        self.seen = {e: {} for e in ENGS}
        self.dbufs = []
        self.bufs = [t.b for t in persistent]
        self.n = 0
        self.rr = {}
        self.pending = {}
        self.seq = 0

    def tile(self, shape, dt, name=None):
        self.n += 1
        name = f"{self.name}_{name or 't'}{self.n}"
        t = Tile(self.es.enter_context(self.nc.sbuf_tensor(name, list(shape), dt)), name)
        self.bufs.append(t.b)
        return t

    def pool(self, n, shape, dt, name):
        return [self.tile(shape, dt, f"{name}{i}_") for i in range(n)]

    def nxt(self, pool):
        k = id(pool)
        i = self.rr.get(k, 0)
        self.rr[k] = i + 1
        return pool[i % len(pool)]

    def _deps(self, eng, reads, writes):
        waits = {}

        def need(tok):
            if tok is None:
                return
            s, v = tok
            if s is self.sem[eng] and (eng == "pe" or not SAME_ENGINE_SYNC):
                return
            k = id(s)
            if self.seen[eng].get(k, 0) < v:
                self.seen[eng][k] = v
                waits[k] = (s, v)

        for b in reads:
            need(b.w)
            if b.name.startswith("ps"):
                for t in b.r.values():
                    if t[0] is not self.sem[eng]:
                        need(t)
        for b in writes:
            need(b.w)
            for t in b.r.values():
                need(t)
        return list(waits.values())

    def op(self, eng, fn, reads=(), writes=(), inc=True):
        reads = [x.b if isinstance(x, Tile) else x for x in reads]
        writes = [x.b if isinstance(x, Tile) else x for x in writes]
        waits = self._deps(eng, reads, writes)
        tok = None
        if not inc:
            self.pending.setdefault(eng, []).extend(reads)
        if inc:
            reads = reads + self.pending.pop(eng, [])
            self.cnt[eng] += 1
            tok = (self.sem[eng], self.cnt[eng])
            for b in writes:
                b.w = tok
                b.r = {}
            for b in reads:
                b.r[id(tok[0])] = tok
        self.seq += 1
        self.ops[eng].append((waits, fn, tok, 1, self.seq))

    def dma(self, q, out_ap, in_ap, reads=(), writes=()):
        reads = [x.b if isinstance(x, Tile) else x for x in reads]
        writes = [x.b if isinstance(x, Tile) else x for x in writes]
        sb = writes[0] if writes else reads[0]
        if sb.dsem is None:
            sb.dsem = self.nc.alloc_semaphore(name=f"{self.name}_d{len(self.dbufs)}")
            self.allsems.append(sb.dsem)
            self.dbufs.append(sb)
        waits = self._deps(q, reads, writes)
        sb.dcnt += 1
        tok = (sb.dsem, 16 * sb.dcnt)
        for b in writes:
            b.w = tok
            b.r = {}
        for b in reads:
            b.r[id(tok[0])] = tok
        self.seq += 1
        self.ops[q].append((waits, lambda e: e.dma_start(out=out_ap, in_=in_ap), tok, 16, self.seq))

    def mm(self, ot, out_ap, lhsT, rhs, start, stop, reads=(), inc=None):
        if inc is None:
            inc = stop
        self.op("pe", lambda e: e.matmul(out_ap, lhsT, rhs, start=start, stop=stop),
                reads=reads, writes=[ot] if (start or inc) else [], inc=inc)

    def act(self, out_ap, in_ap, func, reads, writes, bias=None, scale=1.0):
        if bias is None:
            fn = lambda e: e.activation(out=out_ap, in_=in_ap, func=func, scale=scale)
        else:
            fn = lambda e: e.activation(out=out_ap, in_=in_ap, func=func, bias=bias, scale=scale)
        self.op("act", fn, reads, writes)

    def tt(self, out_ap, in0, in1, op, reads, writes, eng="dve"):
        self.op(eng, lambda e: e.tensor_tensor(out=out_ap, in0=in0, in1=in1, op=op), reads, writes)

    def ts(self, out_ap, in0, s1, s2, op0, op1, reads, writes, eng="dve"):
        if op1 is None:
            fn = lambda e: e.tensor_scalar(out=out_ap, in0=in0, scalar1=s1, scalar2=None, op0=op0)
        else:
            fn = lambda e: e.tensor_scalar(out=out_ap, in0=in0, scalar1=s1, scalar2=s2, op0=op0, op1=op1)
        self.op(eng, fn, reads, writes)

    def stt(self, out_ap, in0, scalar, in1, op0, op1, reads, writes, eng="dve"):
        self.op(eng, lambda e: e.scalar_tensor_tensor(out=out_ap, in0=in0, scalar=scalar, in1=in1, op0=op0, op1=op1),
                reads, writes)

    def recip(self, out_ap, in_ap, reads, writes):
        self.op("dve", lambda e: e.reciprocal(out=out_ap, in_=in_ap), reads, writes)

    def rsq(self, out_ap, in_ap, bias, reads, writes):
        self.act(out_ap, in_ap, AF.Sqrt, reads, writes, bias=float(bias))
        self.recip(out_ap, out_ap, writes, writes)

    def memset(self, out_ap, val, writes, eng="pool"):
        self.op(eng, lambda e: e.memset(out_ap, val), (), writes)

    def finish(self):
        import os
        only = os.environ.get("MK_PHASES")
        if only is not None and self.name.split("_")[0] not in only.split(","):
            for b in self.bufs:
                b.w = None
                b.r = {}
                b.dsem = None
                b.dcnt = 0
            for sm_ in self.allsems:
                self.nc.release_semaphore(sm_)
            self.es.close()
            return
        lim = None
        for item in os.environ.get("MK_LIMIT", "").split(","):
            if item.startswith(self.name.split("_")[0] + ":"):
                lim = int(item.split(":")[1])
        if lim is not None:
            for e in ENGS:
                self.ops[e] = [o for o in self.ops[e] if o[4] <= lim]
            final = {}
            for e in ENGS:
                for o in self.ops[e]:
                    if o[3] == 16:
                        final[id(o[2][0])] = max(final.get(id(o[2][0]), 0), o[2][1])
            for b in self.dbufs:
                if id(b.dsem) in final:
                    self.ops["sp"].append(([(b.dsem, final[id(b.dsem)])], None, None, 0, 0))
            print("phase", self.name, "limited to", lim, "of", self.seq)
        else:
            for b in self.dbufs:
                self.ops["sp"].append(([(b.dsem, 16 * b.dcnt)], None, None, 0, 0))
        ops = self.ops

        def body(name):
            def f(e):
                for waits, fn, tok, incv, _sq in ops[name]:
                    for s, v in waits:
                        e.wait_ge(s, v)
                    if fn is not None:
                        ins = fn(e)
                        if tok is not None:
                            ins.then_inc(tok[0], incv)
            return f

        with self.nc.Block() as block:
            block.tensor(body("pe"))
            block.scalar(body("act"))
            block.vector(body("dve"))
            block.gpsimd(body("pool"))
            block.sync(body("sp"))
        for b in self.bufs:
            b.w = None
            b.r = {}
            b.dsem = None
            b.dcnt = 0
        self.nc.clear_and_free_semaphores(self.allsems)
        with self.nc.Block():
            pass
        self.es.close()


def token_tiles(L, TT=512):
    tiles = [(t0, min(TT, L - t0), False) for t0 in range(0, L, TT)]
    tiles.append((L, CTX, True))
    return tiles


def build(L, depth, dbg=()):
    Lt = L + CTX
    nc = bass.Bass("TRN2", target_bir_lowering=False)
    dt_in = lambda n, s: nc.dram_tensor(n, list(s), F32, kind="ExternalInput").ap()
    xin = dt_in("xin", [D, Lt])
    cT = dt_in("cT", [128, NK, 2])
    smalls = dt_in("smalls", [depth, 128, NS])
    ropec = dt_in("ropec", [64, L])
    ropes = dt_in("ropes", [64, L])
    rmat = dt_in("rmat", [64, 64])
    masks = dt_in("masks", [128, 2, 512])
    w_ada = dt_in("w_ada", [depth, D, 6 * D])
    w_in = dt_in("w_in", [depth, D, INC])
    w_conv_out = dt_in("w_conv_out", [depth, 512, D])
    w_q_up = dt_in("w_q_up", [depth, 512, 1536])
    w_kv_up = dt_in("w_kv_up", [depth, 256, 2048])
    w_mla_out = dt_in("w_mla_out", [depth, 1024, D])
    w_gqa_out = dt_in("w_gqa_out", [depth, 512, D])
    w_out = dt_in("w_out", [depth, D, D])
    w_ffn_in = dt_in("w_ffn_in", [depth, D, 2 * DFF])
    w_ffn_out = dt_in("w_ffn_out", [depth, DFF, D])
    outT = nc.dram_tensor("outT", [D, L], F32, kind="ExternalOutput").ap()

    def scr(n, s, dt=BF16):
        kind = "ExternalOutput" if n in dbg else "Internal"
        return nc.dram_tensor(n, list(s), dt, kind=kind).ap()

    x_s = scr("x_s", [D, Lt], F32)
    h_s = scr("h_s", [D, Lt])
    qn_s = scr("qn_s", [8, 128, Lt])
    qr_s = scr("qr_s", [8, 64, Lt])
    kn_s = scr("kn_s", [8, 128, Lt])
    kr_s = scr("kr_s", [64, Lt])
    mv_s = scr("mv_s", [Lt, 1024])
    gq_s = scr("gq_s", [8, 64, Lt])
    gk_s = scr("gk_s", [2, 64, Lt])
    gv_s = scr("gv_s", [Lt, 128])
    u_s = scr("u_s", [512, Lt])
    cv_s = scr("cv_s", [512, Lt])
    om_s = scr("om_s", [1024, Lt])
    og_s = scr("og_s", [512, Lt])
    wm_b = scr("wm_b", [depth, 16, 128, 8192])
    wo_b = scr("wo_b", [depth, 16, 128, 2048])
    wfi_b = scr("wfi_b", [depth, 88, 128, 2048])
    wfo_b = scr("wfo_b", [depth, 32, 128, 22 * 128])
    wi_b = scr("wi_b", [depth, 128, NK * 2624])
    wq_b = scr("wq_b", [depth, 128, 4 * 1536])
    wkk_b = scr("wkk_b", [depth, 128, 2 * 1024])
    wkv_b = scr("wkv_b", [depth, 128, 2 * 1024])
    mod_dbg = scr("mod_dbg", [depth, 128, 96, 2], F32) if "mod_dbg" in dbg else None

    tiles = token_tiles(L)
    glob = ExitStack()

    def gtile(name, shape, dt):
        return Tile(glob.enter_context(nc.sbuf_tensor(name, list(shape), dt)), name)

    PS = [Tile(glob.enter_context(nc.psum_tensor(f"ps{i}", [128, 512], F32)), f"ps{i}") for i in range(8)]
    sm = gtile("sm", [128, depth, NS], F32)
    mod = [gtile(f"mod{i}", [128, 96, 2], F32) for i in range(depth)]
    A1 = [gtile(f"A1_{i}", [128, NK, 2], F32) for i in range(depth)]
    A2 = [gtile(f"A2_{i}", [128, NK, 2], F32) for i in range(depth)]
    gq_sc = gtile("gq_sc", [128, depth, 4], F32)
    gkv_sc = gtile("gkv_sc", [128, depth, 2], F32)
    ones_b = gtile("ones_b", [128, 128], BF16)
    csb = gtile("csb", [128, NK, 2], BF16)
    ones_f = gtile("ones_f", [128, 128], F32)
    rm_b = gtile("rm_b", [64, 64], BF16)
    mk_b = gtile("mk_b", [128, 2, 512], BF16)
    persistent = PS + [sm, csb, gq_sc, gkv_sc, ones_b, ones_f, rm_b, mk_b] + mod + A1 + A2

    psn = [0]

    def ps():
        psn[0] += 1
        return PS[psn[0] % 8]

    pan = [0, 0]

    def psa():
        pan[0] += 1
        return PS[pan[0] % 4]

    def pss():
        pan[1] += 1
        return PS[4 + pan[1] % 4]

    def run_all(g):
        for _ in g:
            pass

    def gen_prep(ph, l, cast_engs):
        stg_pool = ph.pool(2, [128, 2048], F32, "pstg")
        cb_pool = ph.pool(2, [128, 2048], BF16, "pcb")
        ci = [0]
        pend = []

        def flush():
            while pend:
                dst_ap, src, cb = pend.pop(0)
                ph.dma("sp", dst_ap, src, reads=[cb])

        def piece(src_ap, n, in_view, out_view, dst_ap, dst_view):
            st = ph.nxt(stg_pool)
            ph.dma("sp", in_view(st[:, :n]), src_ap, writes=[st])
            flush()
            cb = ph.nxt(cb_pool)
            ci[0] += 1
            e = cast_engs[ci[0] % len(cast_engs)]
            o_ap, i_ap = out_view(cb[:, :n]), in_view(st[:, :n])
            if e == "act":
                ph.act(o_ap, i_ap, AF.Copy, [st], [cb])
            else:
                ph.op(e, lambda en, o_ap=o_ap, i_ap=i_ap: en.tensor_copy(out=o_ap, in_=i_ap), [st], [cb])
            pend.append((dst_ap, dst_view(cb[:, :n]), cb))

        kp = lambda w: w.rearrange("(k p) n -> p k n", p=128)
        kc = lambda ap: ap.rearrange("p (k c) -> p k c", c=128)
        ident = lambda ap: ap
        kfc = lambda nk: (lambda ap: ap.rearrange("p (k f c) -> p k f c", k=nk, c=128))
        fkc = lambda nk: (lambda ap: ap.rearrange("p (f k c) -> p k f c", k=nk, c=128))
        pfn = lambda nf: (lambda ap: ap.rearrange("p (f n) -> p f n", f=nf))
        yield
        ada_pend = []

        def ada_mm(cb, ch):
            pt = ps()
            for k in range(NK):
                ph.mm(pt, pt[:, 0:2], cb[:, k * 128:(k + 1) * 128], csb[:, k, :], k == 0, k == NK - 1, reads=[cb, csb])
            ph.ts(mod[l][:, ch, :], pt[:, 0:2], sm[:, l, S_BADA + ch:S_BADA + ch + 1], None, ALU.add, None, [pt, sm], [mod[l]])

        for ch in range(96):
            st = ph.nxt(stg_pool)
            ph.dma("sp", kc(st[:, :2048]), kp(w_ada[l])[:, :, ch * 128:(ch + 1) * 128], writes=[st])
            cb = ph.nxt(cb_pool)
            ci[0] += 1
            e = cast_engs[ci[0] % len(cast_engs)]
            o_ap, i_ap = cb[:, :2048], st[:, :2048]
            if e == "act":
                ph.act(o_ap, i_ap, AF.Copy, [st], [cb])
            else:
                ph.op(e, lambda en, o_ap=o_ap, i_ap=i_ap: en.tensor_copy(out=o_ap, in_=i_ap), [st], [cb])
            if ada_pend:
                ada_mm(*ada_pend.pop(0))
            ada_pend.append((cb, ch))
            yield
        while ada_pend:
            ada_mm(*ada_pend.pop(0))
        for (A, g0, j) in ((A1, S_GMIX, 1), (A2, S_GFFN, 4)):
            for col in range(2):
                ph.ts(A[l][:, :, col], mod[l][:, j * 16:(j + 1) * 16, col], 1.0, float(np.sqrt(D)), ALU.add, ALU.mult, [mod[l]], [A[l]])
                ph.tt(A[l][:, :, col], A[l][:, :, col], sm[:, l, g0:g0 + 16], ALU.mult, [A[l], sm], [A[l]])
        if mod_dbg is not None:
            ph.dma("sp", mod_dbg[l], mod[l][:], reads=[mod[l]])
        wiv = wi_b[l].rearrange("p (k n) -> p k n", n=2624)
        for c0 in range(0, 2624, 128):
            n = min(128, 2624 - c0)
            v = (lambda n: (lambda ap: ap.rearrange("p (k c) -> p k c", c=n)))(n)
            piece(kp(w_in[l])[:, :, c0:c0 + n], 16 * n, v, v, wiv[:, :, c0:c0 + n], v)
            yield
        wqv = wq_b[l].rearrange("p (k n) -> p k n", n=1536)
        v512 = lambda ap: ap.rearrange("p (k c) -> p k c", c=512)
        for c0 in range(0, 1536, 512):
            piece(kp(w_q_up[l])[:, :, c0:c0 + 512], 2048, v512, v512, wqv[:, :, c0:c0 + 512], v512)
            yield
        hd = lambda ap: ap.rearrange("p (h d) -> p h d", d=128)
        for two, dstw in ((0, wkk_b), (1, wkv_b)):
            for k in range(2):
                piece(kp(w_kv_up[l]).rearrange("p k (h two d) -> p k h two d", two=2, d=128)[:, k, :, two, :], 1024, hd, hd,
                      dstw[l][:, k * 1024:(k + 1) * 1024], ident)
                yield
        wmv = wm_b[l].rearrange("f p n -> p f n")
        for fc in range(16):
            for g in range(3):
                piece(kp(w_in[l])[:, :, 2624 + g * 2048 + fc * 128:2624 + g * 2048 + (fc + 1) * 128], 2048, kc, kc,
                      wm_b[l, fc][:, g * 2048:(g + 1) * 2048], ident)
                yield
        for f4 in range(4):
            piece(kp(w_conv_out[l])[:, :, f4 * 512:(f4 + 1) * 512], 2048, kfc(4), fkc(4), wmv[:, f4 * 4:(f4 + 1) * 4, 6144:6656], pfn(4))
            yield
            piece(kp(w_gqa_out[l])[:, :, f4 * 512:(f4 + 1) * 512], 2048, kfc(4), fkc(4), wmv[:, f4 * 4:(f4 + 1) * 4, 7680:8192], pfn(4))
            yield
        for f2 in range(8):
            piece(kp(w_mla_out[l])[:, :, f2 * 256:(f2 + 1) * 256], 2048, kfc(8), fkc(8), wmv[:, f2 * 2:(f2 + 1) * 2, 6656:7680], pfn(2))
            yield
        for fc in range(16):
            piece(kp(w_out[l])[:, :, fc * 128:(fc + 1) * 128], 2048, kc, kc, wo_b[l, fc], ident)
            yield
        for j in range(NFF):
            for two in range(2):
                piece(kp(w_ffn_in[l])[:, :, two * DFF + j * 128:two * DFF + (j + 1) * 128], 2048, kc, kc, wfi_b[l, 2 * j + two], ident)
                yield
        for half in range(2):
            for fc in range(16):
                for kk in range(2):
                    r0 = half * 2816 + kk * 1408
                    piece(kp(w_ffn_out[l][r0:r0 + 1408, :])[:, :, fc * 128:(fc + 1) * 128], 1408, kc, kc,
                          wfo_b[l, fc * 2 + half][:, kk * 1408:(kk + 1) * 1408], ident)
                    yield
        flush()

    ph = Phase(nc, "p0", persistent)
    ph.memset(ones_b[:], 1.0, [ones_b])
    ph.memset(ones_f[:], 1.0, [ones_f])
    stg_pool = ph.pool(3, [128, 8192], F32, "stg")
    cb_pool = ph.pool(3, [128, 8192], BF16, "cb")
    ci = [0]

    def cast(out_ap, in_ap, reads, writes):
        ci[0] += 1
        e = ("dve", "act", "pool")[ci[0] % 3]
        if e == "act":
            ph.act(out_ap, in_ap, AF.Copy, reads, writes)
        else:
            ph.op(e, lambda en: en.tensor_copy(out=out_ap, in_=in_ap), reads, writes)

    def stage_cast(src_ap, n_in, in_view, out_view, n_out, q="sp"):
        st = ph.nxt(stg_pool)
        ph.dma(q, in_view(st[:, :n_in]), src_ap, writes=[st])
        cb = ph.nxt(cb_pool)
        cast(out_view(cb[:, :n_out]), in_view(st[:, :n_in]), [st], [cb])
        return cb

    st = ph.nxt(stg_pool)
    ph.dma("sp", st[:64, :64], rmat, writes=[st])
    ph.op("dve", lambda en, st=st: en.tensor_copy(out=rm_b[:], in_=st[:64, :64]), [st], [rm_b])
    st = ph.nxt(stg_pool)
    ph.dma("sp", st[:, :1024].rearrange("p (a b) -> p a b", a=2), masks, writes=[st])
    ph.op("dve", lambda en, st=st: en.tensor_copy(out=mk_b[:], in_=st[:, :1024].rearrange("p (a b) -> p a b", a=2)), [st], [mk_b])
    ph.dma("sp", sm[:], smalls.rearrange("l p n -> p l n"), writes=[sm])
    cs = ph.tile([128, NK, 2], F32, "cs")
    ph.dma("sp", cs[:], cT, writes=[cs])
    ph.act(csb[:], cs[:], AF.Silu, [cs], [csb])
    for l in range(depth):
        ph.ts(gq_sc[:, l, :], sm[:, l, S_GQA:S_GQA + 4], float(np.sqrt(512.0)), None, ALU.mult, None, [sm], [gq_sc])
        ph.ts(gkv_sc[:, l, :], sm[:, l, S_GKV:S_GKV + 2], float(np.sqrt(256.0)), None, ALU.mult, None, [sm], [gkv_sc])
    run_all(gen_prep(ph, 0, ("dve", "act", "pool")))
    ph.finish()

    for l in range(depth):
        last = (l == depth - 1)
        xsrc = xin if l == 0 else x_s
        lsm = lambda c0, n=1: sm[:, l, c0:c0 + n]

        ph = Phase(nc, f"p1_{l}", persistent)
        wi = ph.tile([128, NK, 2624], BF16, "wi")
        wq = ph.tile([128, 4, 1536], BF16, "wq")
        wkk = ph.tile([128, 2, 1024], BF16, "wkk")
        wkv = ph.tile([128, 2, 1024], BF16, "wkv")
        wiv = wi_b[l].rearrange("p (k n) -> p k n", n=2624)
        for k in range(0, NK, 4):
            ph.dma("sp" if (k // 4) % 2 else "act", wi[:, k:k + 4, :], wiv[:, k:k + 4, :], writes=[wi])
        ph.dma("sp", wq[:], wq_b[l].rearrange("p (k n) -> p k n", n=1536), writes=[wq])
        ph.dma("sp", wkk[:], wkk_b[l].rearrange("p (k n) -> p k n", n=1024), writes=[wkk])
        ph.dma("act", wkv[:], wkv_b[l].rearrange("p (k n) -> p k n", n=1024), writes=[wkv])
        xc_pool = ph.pool(4, [128, 512], F32, "xc")
        sq_pool = ph.pool(3, [128, 512], BF16, "sq")
        rstd_pool = ph.pool(2, [128, 512], F32, "rstd")
        hT_pool = ph.pool(2, [128, NK, 512], BF16, "hT")
        f32_pool = ph.pool(4, [128, 512], F32, "f32")
        b16_pool = ph.pool(6, [128, 512], BF16, "b16")
        n32_pool = ph.pool(1, [128, 4, 512], F32, "n32")
        nb_pool = ph.pool(2, [128, 4, 512], BF16, "nb")
        vb_pool = ph.pool(2, [128, 1024], BF16, "vb")
        cs_pool = ph.pool(2, [64, 2, 512], F32, "cs")
        xsv = xsrc.rearrange("(k p) t -> p k t", p=128)
        hsv = h_s.rearrange("(k p) t -> p k t", p=128)

        def rms_stats(srcs, n, T, eps_n):
            pt = ps()
            for i, src in enumerate(srcs):
                ap, tl = src() if callable(src) else src
                sq = ph.nxt(sq_pool)
                ph.act(sq[:, :T], ap, AF.Square, [tl], [sq])
                ph.mm(pt, pt[:, :T], ones_b[:], sq[:, :T], i == 0, i == len(srcs) - 1, reads=[sq, ones_b], inc=True)
            r = ph.nxt(rstd_pool)
            ph.rsq(r[:, :T], pt[:, :T], eps_n, [pt], [r])
            return r

        def rope_store(pt, np_, T, t0, is_ctx, dst_ap, cst):
            xb = ph.nxt(b16_pool)
            ph.act(xb[:np_, :T], pt[:np_, :T], AF.Copy, [pt], [xb])
            if is_ctx:
                ph.dma("sp", dst_ap, xb[:np_, :T], reads=[xb])
                return
            p2 = ps()
            ph.mm(p2, p2[:64, :T], rm_b[:], xb[:64, :T], True, True, reads=[xb, rm_b])
            t1 = ph.nxt(f32_pool)
            ph.tt(t1[:64, :T], pt[:64, :T], cst[:, 0, :T], ALU.mult, [pt, cst, xb], [t1])
            t2 = ph.nxt(f32_pool)
            ph.tt(t2[:64, :T], p2[:64, :T], cst[:, 1, :T], ALU.mult, [p2, cst], [t2])
            ob = ph.nxt(b16_pool)
            ph.tt(ob[:64, :T], t1[:64, :T], t2[:64, :T], ALU.add, [t1, t2], [ob], eng="dve" if os.environ.get("MK_T2") else "pool")
            ph.dma("sp", dst_ap, ob[:64, :T], reads=[ob])

        def norm_tile(t0, T, is_ctx):
            col = 1 if is_ctx else 0
            def ldx(k, t0=t0, T=T):
                def f():
                    xc = ph.nxt(xc_pool)
                    ph.dma("sp", xc[:, :T], xsv[:, k, t0:t0 + T], writes=[xc])
                    return xc[:, :T], xc
                return f
            rstd = rms_stats([ldx(k) for k in range(NK)], D, T, D * EPS)
            hT = ph.nxt(hT_pool)
            for k in range(NK):
                xc = ph.nxt(xc_pool)
                ph.dma("sp", xc[:, :T], xsv[:, k, t0:t0 + T], writes=[xc])
                ph.stt(xc[:, :T], xc[:, :T], A1[l][:, k, col:col + 1], rstd[:, :T], ALU.mult, ALU.mult, [xc, A1[l], rstd], [xc])
                ph.act(hT[:, k, :T], xc[:, :T], AF.Identity, [xc, mod[l]], [hT], bias=mod[l][:, k, col:col + 1])
            ph.dma("sp", hsv[:, :, t0:t0 + T], hT[:, :, :T], reads=[hT])
            if not is_ctx:
                cst = ph.nxt(cs_pool)
                ph.dma("sp", cst[:, 0, :T], ropec[:, t0:t0 + T], writes=[cst])
                ph.dma("sp", cst[:, 1, :T], ropes[:, t0:t0 + T], writes=[cst])
            else:
                cst = None
            return hT, cst

        nxt_norm = norm_tile(*tiles[0])
        for ti, (t0, T, is_ctx) in enumerate(tiles):
            col = 1 if is_ctx else 0
            skip_q = is_ctx and last
            hT, cst = nxt_norm
            if ti + 1 < len(tiles):
                nxt_norm = norm_tile(*tiles[ti + 1])

            def proj(c0, m):
                pt = ps()
                for k in range(NK):
                    ph.mm(pt, pt[:m, :T], wi[:, k, c0:c0 + m], hT[:, k, :T], k == 0, k == NK - 1, reads=[wi, hT])
                return pt

            c32 = ph.nxt(n32_pool)
            for c in range(2):
                pt = proj(c * 128, 128)
                ph.act(c32[:, c, :T], pt[:, :T], AF.Copy, [pt], [c32])
            r = rms_stats([(c32[:, c, :T], c32) for c in range(2)], 256, T, 256 * EPS)
            cn = ph.nxt(nb_pool)
            for c in range(2):
                ph.stt(cn[:, c, :T], c32[:, c, :T], gkv_sc[:, l, c:c + 1], r[:, :T], ALU.mult, ALU.mult, [c32, gkv_sc, r], [cn])
            for h in range(8):
                pt = ps()
                for k in range(2):
                    ph.mm(pt, pt[:, :T], wkk[:, k, h * 128:(h + 1) * 128], cn[:, k, :T], k == 0, k == 1, reads=[wkk, cn])
                ob = ph.nxt(b16_pool)
                ph.act(ob[:, :T], pt[:, :T], AF.Copy, [pt], [ob])
                ph.dma("sp", kn_s[h, :, t0:t0 + T], ob[:, :T], reads=[ob])
            for tc in range(T // 128):
                vb = ph.nxt(vb_pool)
                for hv in range(2):
                    pt = ps()
                    for k in range(2):
                        ph.mm(pt, pt[:, :], cn[:, k, tc * 128:(tc + 1) * 128], wkv[:, k, hv * 512:(hv + 1) * 512], k == 0, k == 1,
                              reads=[wkv, cn])
                    ph.act(vb[:, hv * 512:(hv + 1) * 512], pt[:, :], AF.Copy, [pt], [vb])
                ph.dma("sp", mv_s[t0 + tc * 128:t0 + (tc + 1) * 128, :], vb[:], reads=[vb])
            pt = proj(256, 64)
            rope_store(pt, 64, T, t0, is_ctx, kr_s[:, t0:t0 + T], cst)
            for g in range(2):
                pt = proj(256 if os.environ.get("MK_T1") else 320 + g * 64, 64)
                rope_store(pt, 64, T, t0, is_ctx, gk_s[g, :, t0:t0 + T], cst)
            pt = ps()
            for tc in range(T // 128):
                for k in range(NK):
                    ph.mm(pt, pt[:, tc * 128:(tc + 1) * 128], hT[:, k, tc * 128:(tc + 1) * 128], wi[:, k, 448:576], k == 0, k == NK - 1,
                          reads=[wi, hT], inc=(k == NK - 1 and tc == T // 128 - 1))
            ob = ph.nxt(b16_pool)
            ph.act(ob[:, :T], pt[:, :T], AF.Copy, [pt], [ob])
            ph.dma("sp", gv_s[t0:t0 + T, :].rearrange("(c p) d -> p c d", p=128), ob[:, :T].rearrange("p (c d) -> p c d", d=128), reads=[ob])
            if skip_q:
                continue
            q32 = ph.nxt(n32_pool)
            for c in range(4):
                pt = proj(576 + c * 128, 128)
                ph.act(q32[:, c, :T], pt[:, :T], AF.Copy, [pt], [q32])
            r = rms_stats([(q32[:, c, :T], q32) for c in range(4)], 512, T, 512 * EPS)
            qn = ph.nxt(nb_pool)
            for c in range(4):
                ph.stt(qn[:, c, :T], q32[:, c, :T], gq_sc[:, l, c:c + 1], r[:, :T], ALU.mult, ALU.mult, [q32, gq_sc, r], [qn])
            for h in range(8):
                pt = ps()
                for k in range(4):
                    ph.mm(pt, pt[:, :T], wq[:, k, h * 192:h * 192 + 128], qn[:, k, :T], k == 0, k == 3, reads=[wq, qn])
                ob = ph.nxt(b16_pool)
                ph.act(ob[:, :T], pt[:, :T], AF.Copy, [pt], [ob])
                ph.dma("sp", qn_s[h, :, t0:t0 + T], ob[:, :T], reads=[ob])
                pt = ps()
                for k in range(4):
                    ph.mm(pt, pt[:64, :T], wq[:, k, h * 192 + 128:h * 192 + 192], qn[:, k, :T], k == 0, k == 3, reads=[wq, qn])
                rope_store(pt, 64, T, t0, is_ctx, qr_s[h, :, t0:t0 + T], cst)
            for h in range(8):
                pt = proj(1088 + h * 64, 64)
                rope_store(pt, 64, T, t0, is_ctx, gq_s[h, :, t0:t0 + T], cst)
            for c in range(4):
                pb = proj(2112 + c * 128, 128)
                sg = ph.nxt(f32_pool)
                ph.act(sg[:, :T], pb[:, :T], AF.Sigmoid, [pb], [sg])
                pa = proj(1600 + c * 128, 128)
                ob = ph.nxt(b16_pool)
                ph.tt(ob[:, :T], pa[:, :T], sg[:, :T], ALU.mult, [pa, sg], [ob])
                ph.dma("sp", u_s[c * 128:(c + 1) * 128, t0:t0 + T], ob[:, :T], reads=[ob])
        ph.finish()

        ph = Phase(nc, f"p2x_{l}", persistent)
        nkc = Lt // 128
        p_pool = ph.pool(6, [128, 512], BF16, "p")
        cvv = cv_s.rearrange("(c p) t -> p c t", p=128)

        def gen_mla():
            kr = ph.tile([64, Lt], BF16, "kr")
            ph.dma("sp", kr[:], kr_s, writes=[kr])
            kn_pool = ph.pool(2, [128, Lt], BF16, "kn")
            v_pool = ph.pool(2, [128, nkc, 128], BF16, "v")
            qn_pool = ph.pool(2, [128, Lt], BF16, "qn")
            qr_pool = ph.pool(2, [64, Lt], BF16, "qr")
            rc_pool = ph.pool(2, [128, 512], F32, "rc")
            o_pool = ph.pool(2, [128, 512], BF16, "o")
            sc_mla = float(192.0 ** -0.5)
            yield
            for h in range(8):
                kn = ph.nxt(kn_pool)
                v = ph.nxt(v_pool)
                qn = ph.nxt(qn_pool)
                qr = ph.nxt(qr_pool)
                ph.dma("sp", kn[:], kn_s[h], writes=[kn])
                ph.dma("sp", v[:], mv_s[:, h * 128:(h + 1) * 128].rearrange("(c p) d -> p c d", p=128), writes=[v])
                ph.dma("sp", qn[:], qn_s[h], writes=[qn])
                ph.dma("sp", qr[:], qr_s[h], writes=[qr])
                for (t0, T, is_ctx) in tiles:
                    if is_ctx and last:
                        continue
                    chunks = [nkc - 2, nkc - 1] if is_ctx else list(range(nkc))
                    po = psa()
                    pd = psa()
                    LA = 2
                    pend = []
                    for i, kc in enumerate(chunks + [None] * LA):
                        if kc is not None:
                            pst = pss()
                            ph.mm(pst, pst[:, :T], kn[:, kc * 128:(kc + 1) * 128], qn[:, t0:t0 + T], True, False, reads=[kn, qn], inc=False)
                            ph.mm(pst, pst[:, :T], kr[:, kc * 128:(kc + 1) * 128], qr[:, t0:t0 + T], False, True, reads=[kr, qr], inc=True)
                            p = ph.nxt(p_pool)
                            ph.act(p[:, :T], pst[:, :T], AF.Exp, [pst], [p], scale=sc_mla)
                            pend.append((i, kc, p))
                        if i >= LA:
                            j, kcj, pj = pend.pop(0)
                            fst, lst = j == 0, j == len(chunks) - 1
                            ph.mm(po, po[:, :T], v[:, kcj, :], pj[:, :T], fst, lst, reads=[v, pj], inc=lst)
                            ph.mm(pd, pd[:, :T], ones_b[:], pj[:, :T], fst, lst, reads=[pj, ones_b], inc=True)
                    rc = ph.nxt(rc_pool)
                    ph.act(rc[:, :T], pd[:, :T], AF.Ln, [pd], [rc])
                    ph.act(rc[:, :T], rc[:, :T], AF.Exp, [rc], [rc], scale=-1.0)
                    o = ph.nxt(o_pool)
                    ph.tt(o[:, :T], po[:, :T], rc[:, :T], ALU.mult, [po, rc], [o])
                    ph.dma("sp", om_s[h * 128:(h + 1) * 128, t0:t0 + T], o[:, :T], reads=[o])
                    yield

        def gen_gqa():
            gk = ph.tile([64, 2, Lt], BF16, "gk")
            gv = ph.tile([128, nkc, 128], BF16, "gv")
            ph.dma("sp", gk[:], gk_s.rearrange("g d t -> d g t"), writes=[gk])
            ph.dma("sp", gv[:], gv_s.rearrange("(c p) d -> p c d", p=128), writes=[gv])
            sinkx = ph.tile([64, 2, 512], F32, "sinkx")
            for g in range(2):
                for hh in range(4):
                    ph.act(sinkx[:, g, hh * 128:(hh + 1) * 128],
                           sm[0:64, l, S_SINK + 4 * g + hh:S_SINK + 4 * g + hh + 1].to_broadcast([64, 128]), AF.Exp, [sm], [sinkx])
            gq_pool = ph.pool(2, [64, 8, 512], BF16, "gq")
            ds_pool = ph.pool(2, [64, 512], F32, "ds")
            o_pool = ph.pool(2, [64, 512], BF16, "o")
            sc_gqa = float(64.0 ** -0.5)
            nlb = L // 128
            ogv = og_s.rearrange("(h d) t -> d h t", d=64)
            yield
            gp_pool = ph.pool(12, [128, 512], BF16, "gp")
            prevB = None

            def stageB(st):
                (t0, qb, g, plist) = st
                po = psa()
                pd = psa()
                for j, (kcj, pj) in enumerate(plist):
                    fst, lst = j == 0, j == len(plist) - 1
                    ph.mm(po, po[:64, :], gv[:, kcj, g * 64:(g + 1) * 64], pj[:], fst, lst, reads=[gv, pj], inc=lst)
                    ph.mm(pd, pd[:64, :], ones_b[:, 0:64], pj[:], fst, lst, reads=[pj, ones_b], inc=True)
                ds = ph.nxt(ds_pool)
                ph.tt(ds[:], pd[:64, :], sinkx[:, g, :], ALU.add, [pd, sinkx], [ds])
                ph.act(ds[:], ds[:], AF.Ln, [ds], [ds])
                ph.act(ds[:], ds[:], AF.Exp, [ds], [ds], scale=-1.0)
                o = ph.nxt(o_pool)
                ph.tt(o[:], po[:64, :], ds[:], ALU.mult, [po, ds], [o])
                ph.dma("sp", ogv[:, 4 * g:4 * g + 4, t0 + qb * 128:t0 + (qb + 1) * 128], o[:].rearrange("d (h t) -> d h t", t=128), reads=[o])

            for (t0, T, is_ctx) in tiles:
                if is_ctx and last:
                    continue
                gq = ph.nxt(gq_pool)
                ph.dma("sp", gq[:, :, :T], gq_s[:, :, t0:t0 + T].rearrange("h d t -> d h t"), writes=[gq])
                for qb in range(T // 128):
                    blk = (t0 + qb * 128) // 128
                    if is_ctx:
                        chunks = [(nkc - 2, None), (nkc - 1, None)]
                    else:
                        chunks = []
                        if blk > 0:
                            chunks.append((blk - 1, 0))
                        chunks.append((blk, None))
                        if blk < nlb - 1:
                            chunks.append((blk + 1, 1))
                        chunks += [(nkc - 2, None), (nkc - 1, None)]
                    for g in range(2):
                        if prevB is not None:
                            stageB(prevB)
                        plist = []
                        for (kc, mi) in chunks:
                            pst = pss()
                            ph.mm(pst, pst[:, :].rearrange("p (h t) -> p h t", t=128), gk[:, g, kc * 128:(kc + 1) * 128], gq[:, 4 * g:4 * g + 4, qb * 128:(qb + 1) * 128], True, True,
                                  reads=[gk, gq])
                            p = ph.nxt(gp_pool)
                            ph.act(p[:], pst[:], AF.Exp, [pst], [p], scale=sc_gqa)
                            if mi is not None:
                                ph.tt(p[:], p[:], mk_b[:, mi, :], ALU.mult, [p, mk_b], [p], eng="pool")
                            plist.append((kc, p))
                        prevB = (t0, qb, g, plist)
                        yield
            if prevB is not None:
                stageB(prevB)
            yield

        def gen_conv():
            u_pool = ph.pool(2, [128, 4, 512 + 30], BF16, "u")
            acc_pool = [ph.pool(2, [128, 512], F32, f"acc{c}") for c in range(4)]
            sq_pool = ph.pool(2, [128, 512], F32, "sq")
            st_pool = ph.pool(1, [128, 3, 512], F32, "st")
            cvo_pool = ph.pool(2, [128, 4, 512], BF16, "cvo")
            usv = u_s.rearrange("(c p) t -> p c t", p=128)
            cvv = cv_s.rearrange("(c p) t -> p c t", p=128)
            yield
            for (t0, T, is_ctx) in tiles:
                if is_ctx and last:
                    continue
                s0, s1 = (L, Lt) if is_ctx else (0, L)
                u = ph.nxt(u_pool)
                lo, hi = max(s0, t0 - 15), min(s1, t0 + T + 15)
                if lo > t0 - 15:
                    ph.memset(u[:, :, 0:15], 0.0, [u], eng="dve")
                if hi < t0 + T + 15:
                    ph.memset(u[:, :, T + 15:T + 30], 0.0, [u], eng="dve")
                ph.dma("sp", u[:, :, lo - (t0 - 15):hi - (t0 - 15)], usv[:, :, lo:hi], writes=[u])
                accs = [ph.nxt(acc_pool[c]) for c in range(4)]
                eng = "dve"
                for j in range(31):
                    for c in range(4):
                        acc = accs[c]
                        if j == 0:
                            ph.ts(acc[:, :T], u[:, c, 0:T], lsm(S_CW + c * 31), lsm(S_CB + c), ALU.mult, ALU.add, [u, sm], [acc], eng=eng)
                        else:
                            ph.stt(acc[:, :T], u[:, c, j:j + T], lsm(S_CW + c * 31 + j), acc[:, :T], ALU.mult, ALU.add, [u, sm, acc], [acc], eng=eng)
                    if j % 2 == 1:
                        yield
                yield
                p1 = ps()
                p2 = ps()
                for c in range(4):
                    ph.mm(p1, p1[:, :T], ones_f[:], accs[c][:, :T], c == 0, c == 3, reads=[accs[c], ones_f], inc=True)
                for c in range(4):
                    sq = ph.nxt(sq_pool)
                    ph.act(sq[:, :T], accs[c][:, :T], AF.Square, [accs[c]], [sq])
                    ph.mm(p2, p2[:, :T], ones_f[:], sq[:, :T], c == 0, c == 3, reads=[sq, ones_f], inc=True)
                st = ph.nxt(st_pool)
                ph.ts(st[:, 0, :T], p1[:, :T], 1.0 / 512, None, ALU.mult, None, [p1], [st])
                ph.tt(st[:, 1, :T], st[:, 0, :T], st[:, 0, :T], ALU.mult, [st], [st])
                ph.stt(st[:, 2, :T], p2[:, :T], 1.0 / 512, st[:, 1, :T], ALU.mult, ALU.subtract, [p2, st], [st])
                ph.rsq(st[:, 2, :T], st[:, 2, :T], EPS, [st], [st])
                cvo = ph.nxt(cvo_pool)
                for c in range(4):
                    ph.tt(accs[c][:, :T], accs[c][:, :T], st[:, 0, :T], ALU.subtract, [accs[c], st], [accs[c]])
                for c in range(4):
                    ph.tt(accs[c][:, :T], accs[c][:, :T], st[:, 2, :T], ALU.mult, [accs[c], st], [accs[c]])
                for c in range(4):
                    ph.act(cvo[:, c, :T], accs[c][:, :T], AF.Silu, [accs[c], sm], [cvo], bias=lsm(S_LNB + c), scale=lsm(S_LNG + c))
                ph.dma("sp", cvv[:, :, t0:t0 + T], cvo[:, :, :T], reads=[cvo])
                yield

        gens = [gen_mla(), gen_gqa(), gen_conv()]
        for g_ in gens:
            next(g_)
        alive = [True, True, True]
        step = 0
        while any(alive):
            step += 1
            order = [0, 1, 2, 2]
            if not alive[0]:
                order = [1, 2, 2, 2]
            for gi in order:
                if alive[gi]:
                    try:
                        next(gens[gi])
                    except StopIteration:
                        alive[gi] = False
        ph.finish()

        ph = Phase(nc, f"p2c_{l}", persistent)
        xt_pool = ph.pool(1, [128, NK, 512], F32, "xt")
        ht_pool = ph.pool(1, [128, NK, 512], BF16, "ht")
        yT_pool = ph.pool(1, [128, NK, 512], BF16, "yT")
        at_pool = ph.pool(1, [128, 22, 512], BF16, "at")
        cv_pool = ph.pool(1, [128, 4, 512], BF16, "cv")
        om_pool = ph.pool(1, [128, 8, 512], BF16, "om")
        og_pool = ph.pool(1, [128, 4, 512], BF16, "og")
        wb_pool = ph.pool(3, [128, 8192], BF16, "wb")
        sg_pool = ph.pool(3, [128, 512], F32, "sg")
        t_pool = ph.pool(3, [128, 512], F32, "tt")
        acc_pool = ph.pool(2, [128, 512], F32, "acc")
        sq_pool = ph.pool(3, [128, 512], BF16, "sq")
        rstd_pool = ph.pool(1, [128, 512], F32, "rstd")
        xdv = x_s.rearrange("(k p) t -> p k t", p=128)
        omv = om_s.rearrange("(k p) t -> p k t", p=128)
        ogv2 = og_s.rearrange("(k p) t -> p k t", p=128)
        wq_i = [0]

        prep = None
        if not last:
            prep = gen_prep(ph, l + 1, ("pool",))
            next(prep)

        def prep_step(nsteps=1):
            nonlocal prep
            for _ in range(nsteps):
                if prep is not None:
                    try:
                        next(prep)
                    except StopIteration:
                        prep = None

        def wload(src_ap, n):
            wb = ph.nxt(wb_pool)
            wq_i[0] += 1
            ph.dma("sp", wb[:, :n], src_ap, writes=[wb])
            if wq_i[0] % 3 != 0:
                prep_step()
            return wb

        for (t0, T, is_ctx) in tiles:
            if is_ctx and last:
                continue
            col = 1 if is_ctx else 0
            xt = ph.nxt(xt_pool)
            ht = ph.nxt(ht_pool)
            yT = ph.nxt(yT_pool)
            at = ph.nxt(at_pool)
            cv = ph.nxt(cv_pool)
            om = ph.nxt(om_pool)
            og = ph.nxt(og_pool)
            for k in range(0, NK, 4):
                ph.dma("sp", xt[:, k:k + 4, :T], xsv[:, k:k + 4, t0:t0 + T], writes=[xt])
            ph.dma("sp", ht[:, :, :T], hsv[:, :, t0:t0 + T], writes=[ht])
            ph.dma("sp", cv[:, :, :T], cvv[:, :, t0:t0 + T], writes=[cv])
            ph.dma("sp", om[:, :, :T], omv[:, :, t0:t0 + T], writes=[om])
            ph.dma("sp", og[:, :, :T], ogv2[:, :, t0:t0 + T], writes=[og])
            for fc in range(16):
                wb = wload(wm_b[l, fc], 8192)
                acc = ph.nxt(acc_pool)
                for g, (src, nk, off) in enumerate(((cv, 4, 6144), (om, 8, 6656), (og, 4, 7680))):
                    pg = ps()
                    for k in range(NK):
                        ph.mm(pg, pg[:, :T], wb[:, g * 2048 + k * 128:g * 2048 + (k + 1) * 128], ht[:, k, :T], k == 0, k == NK - 1, reads=[wb, ht])
                    sg = ph.nxt(sg_pool)
                    ph.act(sg[:, :T], pg[:, :T], AF.Sigmoid, [pg], [sg])
                    py = ps()
                    for k in range(nk):
                        ph.mm(py, py[:, :T], wb[:, off + k * 128:off + (k + 1) * 128], src[:, k, :T], k == 0, k == nk - 1, reads=[wb, src])
                    if g == 0:
                        ph.tt(acc[:, :T], py[:, :T], sg[:, :T], ALU.mult, [py, sg], [acc])
                    else:
                        tq = ph.nxt(t_pool)
                        ph.tt(tq[:, :T], py[:, :T], sg[:, :T], ALU.mult, [py, sg], [tq])
                        if g == 1:
                            ph.tt(acc[:, :T], acc[:, :T], tq[:, :T], ALU.add, [acc, tq], [acc], eng="pool")
                        else:
                            ph.tt(yT[:, fc, :T], acc[:, :T], tq[:, :T], ALU.add, [acc, tq], [yT], eng="pool")
            for f4 in range(4):
                wb = wload(wo_b[l, f4 * 4:(f4 + 1) * 4].rearrange("f p n -> p f n"), 8192)
                for fi in range(4):
                    fc = f4 * 4 + fi
                    po = ps()
                    for k in range(NK):
                        ph.mm(po, po[:, :T], wb[:, fi * 2048 + k * 128:fi * 2048 + (k + 1) * 128], yT[:, k, :T], k == 0, k == NK - 1, reads=[wb, yT])
                    ph.stt(xt[:, fc, :T], po[:, :T], mod[l][:, 2 * 16 + fc, col:col + 1], xt[:, fc, :T], ALU.mult, ALU.add, [po, mod[l], xt], [xt])
            pt = ps()
            for k in range(NK):
                sq = ph.nxt(sq_pool)
                ph.act(sq[:, :T], xt[:, k, :T], AF.Square, [xt], [sq])
                ph.mm(pt, pt[:, :T], ones_b[:], sq[:, :T], k == 0, k == NK - 1, reads=[sq, ones_b], inc=True)
            rstd = ph.nxt(rstd_pool)
            ph.rsq(rstd[:, :T], pt[:, :T], D * EPS, [pt], [rstd])
            for k in range(NK):
                tq = ph.nxt(t_pool)
                ph.stt(tq[:, :T], xt[:, k, :T], A2[l][:, k, col:col + 1], rstd[:, :T], ALU.mult, ALU.mult, [xt, A2[l], rstd], [tq])
                ph.act(ht[:, k, :T], tq[:, :T], AF.Identity, [tq, mod[l]], [ht], bias=mod[l][:, 3 * 16 + k, col:col + 1])
            for half in range(2):
                for j2 in range(0, 22, 2):
                    j = half * 22 + j2
                    wb = wload(wfi_b[l, 2 * j:2 * j + 4].rearrange("f p n -> p f n"), 8192)
                    for jj in range(2):
                        pa = ps()
                        for k in range(NK):
                            ph.mm(pa, pa[:, :T], wb[:, (2 * jj) * 2048 + k * 128:(2 * jj) * 2048 + (k + 1) * 128], ht[:, k, :T], k == 0, k == NK - 1, reads=[wb, ht])
                        sg = ph.nxt(sg_pool)
                        ph.act(sg[:, :T], pa[:, :T], AF.Silu, [pa], [sg])
                        pb = ps()
                        for k in range(NK):
                            ph.mm(pb, pb[:, :T], wb[:, (2 * jj + 1) * 2048 + k * 128:(2 * jj + 1) * 2048 + (k + 1) * 128], ht[:, k, :T], k == 0, k == NK - 1, reads=[wb, ht])
                        ph.tt(at[:, j2 + jj, :T], pb[:, :T], sg[:, :T], ALU.mult, [pb, sg], [at])
                for fc in range(16):
                    wb = wload(wfo_b[l, fc * 2 + half], 22 * 128)
                    po = ps()
                    for k in range(22):
                        ph.mm(po, po[:, :T], wb[:, k * 128:(k + 1) * 128], at[:, k, :T], k == 0, k == 21, reads=[wb, at])
                    ph.stt(xt[:, fc, :T], po[:, :T], mod[l][:, 5 * 16 + fc, col:col + 1], xt[:, fc, :T], ALU.mult, ALU.add, [po, mod[l], xt], [xt])
            for k in range(0, NK, 4):
                ph.dma("sp", xdv[:, k:k + 4, t0:t0 + T], xt[:, k:k + 4, :T], reads=[xt])
        while prep is not None:
            prep_step()
        ph.finish()

    ph = Phase(nc, "fin", persistent)
    xt_pool = ph.pool(2, [128, NK, 512], F32, "xt")
    sq_pool = ph.pool(3, [128, 512], BF16, "sq")
    rstd_pool = ph.pool(2, [128, 512], F32, "rstd")
    gf = ph.tile([128, NK], F32, "gf")
    ph.ts(gf[:], sm[:, 0, S_GFIN:S_GFIN + 16], float(np.sqrt(D)), None, ALU.mult, None, [sm], [gf])
    xdv = x_s.rearrange("(k p) t -> p k t", p=128)
    odv = outT.rearrange("(k p) t -> p k t", p=128)
    for (t0, T, is_ctx) in tiles:
        if is_ctx:
            continue
        xt = ph.nxt(xt_pool)
        for k in range(0, NK, 4):
            ph.dma("sp", xt[:, k:k + 4, :T], xdv[:, k:k + 4, t0:t0 + T], writes=[xt])
        pt = ps()
        for k in range(NK):
            sq = ph.nxt(sq_pool)
            ph.act(sq[:, :T], xt[:, k, :T], AF.Square, [xt], [sq])
            ph.mm(pt, pt[:, :T], ones_b[:], sq[:, :T], k == 0, k == NK - 1, reads=[sq, ones_b], inc=True)
        rstd = ph.nxt(rstd_pool)
        ph.rsq(rstd[:, :T], pt[:, :T], D * EPS, [pt], [rstd])
        for k in range(NK):
            ph.stt(xt[:, k, :T], xt[:, k, :T], gf[:, k:k + 1], rstd[:, :T], ALU.mult, ALU.mult, [xt, gf, rstd], [xt])
        for k in range(0, NK, 4):
            ph.dma("sp", odv[:, k:k + 4, t0:t0 + T], xt[:, k:k + 4, :T], reads=[xt])
    ph.finish()
    glob.close()
    return nc


def host_consts(L):
    GRID_W = 64
    rows = L // GRID_W
    row = np.repeat(np.arange(rows), GRID_W).astype(np.float32)
    colp = np.tile(np.arange(GRID_W), rows).astype(np.float32)
    inv_freq = (np.float32(10000.0) ** (-np.arange(16, dtype=np.float32) / np.float32(16))).astype(np.float32)
    a_row = row[:, None] * inv_freq[None, :]
    a_col = colp[:, None] * inv_freq[None, :]
    ang = np.concatenate([a_row, a_row, a_col, a_col], axis=-1).astype(np.float32)
    ropec = np.ascontiguousarray(np.cos(ang).T.astype(np.float32))
    ropes = np.ascontiguousarray(np.sin(ang).T.astype(np.float32))
    rmat = np.zeros((64, 64), np.float32)
    for m in range(64):
        if m % 32 < 16:
            rmat[m + 16, m] = -1.0
        else:
            rmat[m - 16, m] = 1.0
    j = np.arange(128)[:, None]
    i = np.arange(128)[None, :]
    mA = (j >= i).astype(np.float32)
    mB = (j <= i).astype(np.float32)
    masks = np.stack([np.tile(mA, (1, 4)), np.tile(mB, (1, 4))], axis=1)
    return ropec, ropes, rmat, np.ascontiguousarray(masks)


def pack_smalls(depth, b_ada, g_mix, g_ffn, g_q_a, g_kv_a, conv_b, conv_ln_g, conv_ln_b, conv_w, gqa_sink, g_final):
    sm = np.zeros((depth, 128, NS), np.float32)

    def fm(v):
        return np.asarray(v, np.float32).reshape(-1, 128).T

    for l in range(depth):
        sm[l, :, S_GMIX:S_GMIX + 16] = fm(g_mix[l])
        sm[l, :, S_GFFN:S_GFFN + 16] = fm(g_ffn[l])
        sm[l, :, S_BADA:S_BADA + 96] = fm(b_ada[l])
        sm[l, :, S_GQA:S_GQA + 4] = fm(g_q_a[l])
        sm[l, :, S_GKV:S_GKV + 2] = fm(g_kv_a[l])
        sm[l, :, S_CB:S_CB + 4] = fm(conv_b[l])
        sm[l, :, S_LNG:S_LNG + 4] = fm(conv_ln_g[l])
        sm[l, :, S_LNB:S_LNB + 4] = fm(conv_ln_b[l])
        cw = np.asarray(conv_w[l], np.float32)
        for c in range(4):
            sm[l, :, S_CW + c * 31:S_CW + (c + 1) * 31] = cw[:, c * 128:(c + 1) * 128].T
        sm[l, :, S_SINK:S_SINK + 8] = np.asarray(gqa_sink[l], np.float32)[None, :]
        sm[l, :, S_GFIN:S_GFIN + 16] = fm(g_final)
    return sm


_CACHE = {}


def run(inputs, L, depth, n_cores, dbg=()):
    key = (L, depth, tuple(dbg))
    if key not in _CACHE:
        _CACHE[key] = build(L, depth, dbg)
    nc = _CACHE[key]
    f = lambda a: np.ascontiguousarray(np.asarray(a, np.float32))
    x, c, ctx, c_ctx = f(inputs["x"]), f(inputs["c"]), f(inputs["ctx"]), f(inputs["c_ctx"])
    ropec, ropes, rmat, masks = host_consts(L)
    sm = pack_smalls(depth, *[np.asarray(inputs[k]) for k in ("b_ada", "g_mix", "g_ffn", "g_q_a", "g_kv_a", "conv_b", "conv_ln_g",
                                                               "conv_ln_b", "conv_w", "gqa_sink")], np.asarray(inputs["g_final"]))
    shared = {"smalls": sm, "ropec": ropec, "ropes": ropes, "rmat": rmat, "masks": masks}
    for k in ("w_ada", "w_in", "w_conv_out", "w_q_up", "w_kv_up", "w_mla_out", "w_gqa_out", "w_out", "w_ffn_in", "w_ffn_out"):
        shared[k] = f(inputs[k])[:depth]
    in_maps = []
    for b in range(n_cores):
        m = dict(shared)
        m["xin"] = np.ascontiguousarray(np.concatenate([x[b].T, ctx[b].T], axis=1))
        cc = np.stack([c[b], c_ctx], axis=1)
        m["cT"] = np.ascontiguousarray(cc.reshape(NK, 128, 2).transpose(1, 0, 2))
        in_maps.append(m)
    res = run_bass_kernel_spmd(nc, in_maps, core_ids=list(range(n_cores)))
    return res


def kernel(**inputs):
    B, L, _ = inputs["x"].shape
    res = run(inputs, L, 4, B)
    out = np.stack([np.ascontiguousarray(r["outT"].T) for r in res.results], axis=0)
    return out.astype(np.float32)
```

```python
import numpy as np
import os
from contextlib import ExitStack
import concourse.bass as bass
import concourse.mybir as mybir
from concourse.bass_utils import run_bass_kernel_spmd

F32 = mybir.dt.float32
BF16 = mybir.dt.bfloat16
AF = mybir.ActivationFunctionType
ALU = mybir.AluOpType

D = 2048
NK = 16
CTX = 256
DFF = 5632
NFF = 44
INC = 8768
EPS = 1e-6
ENGS = ["pe", "act", "dve", "pool", "sp"]
SAME_ENGINE_SYNC = True

S_GMIX, S_GFFN, S_BADA, S_GQA, S_GKV, S_CB, S_LNG, S_LNB, S_CW, S_SINK, S_GFIN = 0, 16, 32, 128, 132, 134, 138, 142, 146, 270, 278
NS = 294


class Buf:
    __slots__ = ("name", "w", "r", "dsem", "dcnt")

    def __init__(self, name):
        self.name = name
        self.w = None
        self.r = {}
        self.dsem = None
        self.dcnt = 0


class Tile:
    def __init__(self, t, name):
        self.t = t
        self.b = Buf(name)

    def __getitem__(self, k):
        return self.t[k]


class Phase:
    def __init__(self, nc, name, persistent=()):
        self.nc = nc
        self.name = name
        self.es = ExitStack()
        self.ops = {e: [] for e in ENGS}
        self.sem = {e: nc.alloc_semaphore(name=f"{name}_{e}") for e in ENGS}
        self.allsems = list(self.sem.values())
        self.cnt = {e: 0 for e in ENGS}
        self.seen = {e: {} for e in ENGS}
        self.dbufs = []
        self.bufs = [t.b for t in persistent]
        self.n = 0
        self.rr = {}
        self.pending = {}
        self.seq = 0

    def tile(self, shape, dt, name=None):
        self.n += 1
        name = f"{self.name}_{name or 't'}{self.n}"
        t = Tile(self.es.enter_context(self.nc.sbuf_tensor(name, list(shape), dt)), name)
        self.bufs.append(t.b)
        return t

    def pool(self, n, shape, dt, name):
        return [self.tile(shape, dt, f"{name}{i}_") for i in range(n)]

    def nxt(self, pool):
        k = id(pool)
        i = self.rr.get(k, 0)
        self.rr[k] = i + 1
        return pool[i % len(pool)]

    def _deps(self, eng, reads, writes):
        waits = {}

        def need(tok):
            if tok is None:
                return
            s, v = tok
            if s is self.sem[eng] and (eng == "pe" or not SAME_ENGINE_SYNC):
                return
            k = id(s)
            if self.seen[eng].get(k, 0) < v:
                self.seen[eng][k] = v
                waits[k] = (s, v)

        for b in reads:
            need(b.w)
            if b.name.startswith("ps"):
                for t in b.r.values():
                    if t[0] is not self.sem[eng]:
                        need(t)
        for b in writes:
            need(b.w)
            for t in b.r.values():
                need(t)
        return list(waits.values())

    def op(self, eng, fn, reads=(), writes=(), inc=True):
        reads = [x.b if isinstance(x, Tile) else x for x in reads]
        writes = [x.b if isinstance(x, Tile) else x for x in writes]
        waits = self._deps(eng, reads, writes)
        tok = None
        if not inc:
            self.pending.setdefault(eng, []).extend(reads)
        if inc:
            reads = reads + self.pending.pop(eng, [])
            self.cnt[eng] += 1
            tok = (self.sem[eng], self.cnt[eng])
            for b in writes:
                b.w = tok
                b.r = {}
            for b in reads:
                b.r[id(tok[0])] = tok
        self.seq += 1
        self.ops[eng].append((waits, fn, tok, 1, self.seq))

    def dma(self, q, out_ap, in_ap, reads=(), writes=()):
        reads = [x.b if isinstance(x, Tile) else x for x in reads]
        writes = [x.b if isinstance(x, Tile) else x for x in writes]
        sb = writes[0] if writes else reads[0]
        if sb.dsem is None:
            sb.dsem = self.nc.alloc_semaphore(name=f"{self.name}_d{len(self.dbufs)}")
            self.allsems.append(sb.dsem)
            self.dbufs.append(sb)
        waits = self._deps(q, reads, writes)
        sb.dcnt += 1
        tok = (sb.dsem, 16 * sb.dcnt)
        for b in writes:
            b.w = tok
            b.r = {}
        for b in reads:
            b.r[id(tok[0])] = tok
        self.seq += 1
        self.ops[q].append((waits, lambda e: e.dma_start(out=out_ap, in_=in_ap), tok, 16, self.seq))

    def mm(self, ot, out_ap, lhsT, rhs, start, stop, reads=(), inc=None):
        if inc is None:
            inc = stop
        self.op("pe", lambda e: e.matmul(out_ap, lhsT, rhs, start=start, stop=stop),
                reads=reads, writes=[ot] if (start or inc) else [], inc=inc)

    def act(self, out_ap, in_ap, func, reads, writes, bias=None, scale=1.0):
        if bias is None:
            fn = lambda e: e.activation(out=out_ap, in_=in_ap, func=func, scale=scale)
        else:
            fn = lambda e: e.activation(out=out_ap, in_=in_ap, func=func, bias=bias, scale=scale)
        self.op("act", fn, reads, writes)

    def tt(self, out_ap, in0, in1, op, reads, writes, eng="dve"):
        self.op(eng, lambda e: e.tensor_tensor(out=out_ap, in0=in0, in1=in1, op=op), reads, writes)

    def ts(self, out_ap, in0, s1, s2, op0, op1, reads, writes, eng="dve"):
        if op1 is None:
            fn = lambda e: e.tensor_scalar(out=out_ap, in0=in0, scalar1=s1, scalar2=None, op0=op0)
        else:
            fn = lambda e: e.tensor_scalar(out=out_ap, in0=in0, scalar1=s1, scalar2=s2, op0=op0, op1=op1)
        self.op(eng, fn, reads, writes)

    def stt(self, out_ap, in0, scalar, in1, op0, op1, reads, writes, eng="dve"):
        self.op(eng, lambda e: e.scalar_tensor_tensor(out=out_ap, in0=in0, scalar=scalar, in1=in1, op0=op0, op1=op1),
                reads, writes)

    def recip(self, out_ap, in_ap, reads, writes):
        self.op("dve", lambda e: e.reciprocal(out=out_ap, in_=in_ap), reads, writes)

    def rsq(self, out_ap, in_ap, bias, reads, writes):
        self.act(out_ap, in_ap, AF.Ln, reads, writes, bias=float(bias))
        self.act(out_ap, out_ap, AF.Exp, writes, writes, scale=-0.5)

    def memset(self, out_ap, val, writes, eng="pool"):
        self.op(eng, lambda e: e.memset(out_ap, val), (), writes)

    def finish(self):
        import os
        only = os.environ.get("MK_PHASES")
        if only is not None and self.name.split("_")[0] not in only.split(","):
            for b in self.bufs:
                b.w = None
                b.r = {}
                b.dsem = None
                b.dcnt = 0
            for sm_ in self.allsems:
                self.nc.release_semaphore(sm_)
            self.es.close()
            return
        lim = None
        for item in os.environ.get("MK_LIMIT", "").split(","):
            if item.startswith(self.name.split("_")[0] + ":"):
                lim = int(item.split(":")[1])
        if lim is not None:
            for e in ENGS:
                self.ops[e] = [o for o in self.ops[e] if o[4] <= lim]
            final = {}
            for e in ENGS:
                for o in self.ops[e]:
                    if o[3] == 16:
                        final[id(o[2][0])] = max(final.get(id(o[2][0]), 0), o[2][1])
            for b in self.dbufs:
                if id(b.dsem) in final:
                    self.ops["sp"].append(([(b.dsem, final[id(b.dsem)])], None, None, 0, 0))
            print("phase", self.name, "limited to", lim, "of", self.seq)
        else:
            for b in self.dbufs:
                self.ops["sp"].append(([(b.dsem, 16 * b.dcnt)], None, None, 0, 0))
        ops = self.ops

        def body(name):
            def f(e):
                for waits, fn, tok, incv, _sq in ops[name]:
                    for s, v in waits:
                        e.wait_ge(s, v)
                    if fn is not None:
                        ins = fn(e)
                        if tok is not None:
                            ins.then_inc(tok[0], incv)
            return f

        with self.nc.Block() as block:
            block.tensor(body("pe"))
            block.scalar(body("act"))
            block.vector(body("dve"))
            block.gpsimd(body("pool"))
            block.sync(body("sp"))
        for b in self.bufs:
            b.w = None
            b.r = {}
            b.dsem = None
            b.dcnt = 0
        self.nc.clear_and_free_semaphores(self.allsems)
        with self.nc.Block():
            pass
        self.es.close()


def token_tiles(L, TT=512):
    tiles = [(t0, min(TT, L - t0), False) for t0 in range(0, L, TT)]
    tiles.append((L, CTX, True))
    return tiles


def build(L, depth, dbg=()):
    Lt = L + CTX
    nc = bass.Bass("TRN2", target_bir_lowering=False)
    dt_in = lambda n, s: nc.dram_tensor(n, list(s), F32, kind="ExternalInput").ap()
    xin = dt_in("xin", [D, Lt])
    cT = dt_in("cT", [128, NK, 2])
    smalls = dt_in("smalls", [depth, 128, NS])
    ropec = dt_in("ropec", [64, L])
    ropes = dt_in("ropes", [64, L])
    rmat = dt_in("rmat", [64, 64])
    masks = dt_in("masks", [128, 2, 512])
    w_ada = dt_in("w_ada", [depth, D, 6 * D])
    w_in = dt_in("w_in", [depth, D, INC])
    w_conv_out = dt_in("w_conv_out", [depth, 512, D])
    w_q_up = dt_in("w_q_up", [depth, 512, 1536])
    w_kv_up = dt_in("w_kv_up", [depth, 256, 2048])
    w_mla_out = dt_in("w_mla_out", [depth, 1024, D])
    w_gqa_out = dt_in("w_gqa_out", [depth, 512, D])
    w_out = dt_in("w_out", [depth, D, D])
    w_ffn_in = dt_in("w_ffn_in", [depth, D, 2 * DFF])
    w_ffn_out = dt_in("w_ffn_out", [depth, DFF, D])
    outT = nc.dram_tensor("outT", [D, L], F32, kind="ExternalOutput").ap()

    def scr(n, s, dt=BF16):
        kind = "ExternalOutput" if n in dbg else "Internal"
        return nc.dram_tensor(n, list(s), dt, kind=kind).ap()

    x_s = scr("x_s", [D, Lt], F32)
    h_s = scr("h_s", [D, Lt])
    qn_s = scr("qn_s", [8, 128, Lt])
    qr_s = scr("qr_s", [8, 64, Lt])
    kn_s = scr("kn_s", [8, 128, Lt])
    kr_s = scr("kr_s", [64, Lt])
    mv_s = scr("mv_s", [Lt, 1024])
    gq_s = scr("gq_s", [8, 64, Lt])
    gk_s = scr("gk_s", [2, 64, Lt])
    gv_s = scr("gv_s", [Lt, 128])
    u_s = scr("u_s", [512, Lt])
    cv_s = scr("cv_s", [512, Lt])
    om_s = scr("om_s", [1024, Lt])
    og_s = scr("og_s", [512, Lt])
    wm_b = scr("wm_b", [depth, 16, 128, 8192])
    wo_b = scr("wo_b", [depth, 16, 128, 2048])
    wfi_b = scr("wfi_b", [depth, 88, 128, 2048])
    wfo_b = scr("wfo_b", [depth, 32, 128, 22 * 128])
    wi_b = scr("wi_b", [depth, 128, NK * 2624])
    wq_b = scr("wq_b", [depth, 128, 4 * 1536])
    wkk_b = scr("wkk_b", [depth, 128, 2 * 1024])
    wkv_b = scr("wkv_b", [depth, 128, 2 * 1024])
    mod_dbg = scr("mod_dbg", [depth, 128, 96, 2], F32) if "mod_dbg" in dbg else None

    tiles = token_tiles(L)
    glob = ExitStack()

    def gtile(name, shape, dt):
        return Tile(glob.enter_context(nc.sbuf_tensor(name, list(shape), dt)), name)

    PS = [Tile(glob.enter_context(nc.psum_tensor(f"ps{i}", [128, 512], F32)), f"ps{i}") for i in range(8)]
    sm = gtile("sm", [128, depth, NS], F32)
    mod = [gtile(f"mod{i}", [128, 96, 2], F32) for i in range(depth)]
    A1 = [gtile(f"A1_{i}", [128, NK, 2], F32) for i in range(depth)]
    A2 = [gtile(f"A2_{i}", [128, NK, 2], F32) for i in range(depth)]
    gq_sc = gtile("gq_sc", [128, depth, 4], F32)
    gkv_sc = gtile("gkv_sc", [128, depth, 2], F32)
    ones_b = gtile("ones_b", [128, 128], BF16)
    csb = gtile("csb", [128, NK, 2], BF16)
    ones_f = gtile("ones_f", [128, 128], F32)
    rm_b = gtile("rm_b", [64, 64], BF16)
    mk_b = gtile("mk_b", [128, 2, 512], BF16)
    persistent = PS + [sm, csb, gq_sc, gkv_sc, ones_b, ones_f, rm_b, mk_b] + mod + A1 + A2

    psn = [0]

    def ps():
        psn[0] += 1
        return PS[psn[0] % 8]

    pan = [0, 0]

    def psa():
        pan[0] += 1
        return PS[pan[0] % 4]

    def pss():
        pan[1] += 1
        return PS[4 + pan[1] % 4]

    def run_all(g):
        for _ in g:
            pass

    def gen_prep(ph, l, cast_engs):
        stg_pool = ph.pool(2, [128, 2048], F32, "pstg")
        cb_pool = ph.pool(2, [128, 2048], BF16, "pcb")
        ci = [0]
        pend = []

        def flush():
            while pend:
                dst_ap, src, cb = pend.pop(0)
                ph.dma("sp", dst_ap, src, reads=[cb])

        def piece(src_ap, n, in_view, out_view, dst_ap, dst_view):
            st = ph.nxt(stg_pool)
            ph.dma("sp", in_view(st[:, :n]), src_ap, writes=[st])
            flush()
            cb = ph.nxt(cb_pool)
            ci[0] += 1
            e = cast_engs[ci[0] % len(cast_engs)]
            o_ap, i_ap = out_view(cb[:, :n]), in_view(st[:, :n])
            if e == "act":
                ph.act(o_ap, i_ap, AF.Copy, [st], [cb])
            else:
                ph.op(e, lambda en, o_ap=o_ap, i_ap=i_ap: en.tensor_copy(out=o_ap, in_=i_ap), [st], [cb])
            pend.append((dst_ap, dst_view(cb[:, :n]), cb))

        kp = lambda w: w.rearrange("(k p) n -> p k n", p=128)
        kc = lambda ap: ap.rearrange("p (k c) -> p k c", c=128)
        ident = lambda ap: ap
        kfc = lambda nk: (lambda ap: ap.rearrange("p (k f c) -> p k f c", k=nk, c=128))
        fkc = lambda nk: (lambda ap: ap.rearrange("p (f k c) -> p k f c", k=nk, c=128))
        pfn = lambda nf: (lambda ap: ap.rearrange("p (f n) -> p f n", f=nf))
        yield
        ada_pend = []

        def ada_mm(cb, ch):
            pt = ps()
            for k in range(NK):
                ph.mm(pt, pt[:, 0:2], cb[:, k * 128:(k + 1) * 128], csb[:, k, :], k == 0, k == NK - 1, reads=[cb, csb])
            ph.ts(mod[l][:, ch, :], pt[:, 0:2], sm[:, l, S_BADA + ch:S_BADA + ch + 1], None, ALU.add, None, [pt, sm], [mod[l]])

        for ch in range(96):
            st = ph.nxt(stg_pool)
            ph.dma("sp", kc(st[:, :2048]), kp(w_ada[l])[:, :, ch * 128:(ch + 1) * 128], writes=[st])
            cb = ph.nxt(cb_pool)
            ci[0] += 1
            e = cast_engs[ci[0] % len(cast_engs)]
            o_ap, i_ap = cb[:, :2048], st[:, :2048]
            if e == "act":
                ph.act(o_ap, i_ap, AF.Copy, [st], [cb])
            else:
                ph.op(e, lambda en, o_ap=o_ap, i_ap=i_ap: en.tensor_copy(out=o_ap, in_=i_ap), [st], [cb])
            if ada_pend:
                ada_mm(*ada_pend.pop(0))
            ada_pend.append((cb, ch))
            yield
        while ada_pend:
            ada_mm(*ada_pend.pop(0))
        for (A, g0, j) in ((A1, S_GMIX, 1), (A2, S_GFFN, 4)):
            for col in range(2):
                ph.ts(A[l][:, :, col], mod[l][:, j * 16:(j + 1) * 16, col], 1.0, float(np.sqrt(D)), ALU.add, ALU.mult, [mod[l]], [A[l]])
                ph.tt(A[l][:, :, col], A[l][:, :, col], sm[:, l, g0:g0 + 16], ALU.mult, [A[l], sm], [A[l]])
        if mod_dbg is not None:
            ph.dma("sp", mod_dbg[l], mod[l][:], reads=[mod[l]])
        wiv = wi_b[l].rearrange("p (k n) -> p k n", n=2624)
        for c0 in range(0, 2624, 128):
            n = min(128, 2624 - c0)
            v = (lambda n: (lambda ap: ap.rearrange("p (k c) -> p k c", c=n)))(n)
            piece(kp(w_in[l])[:, :, c0:c0 + n], 16 * n, v, v, wiv[:, :, c0:c0 + n], v)
            yield
        wqv = wq_b[l].rearrange("p (k n) -> p k n", n=1536)
        v512 = lambda ap: ap.rearrange("p (k c) -> p k c", c=512)
        for c0 in range(0, 1536, 512):
            piece(kp(w_q_up[l])[:, :, c0:c0 + 512], 2048, v512, v512, wqv[:, :, c0:c0 + 512], v512)
            yield
        hd = lambda ap: ap.rearrange("p (h d) -> p h d", d=128)
        for two, dstw in ((0, wkk_b), (1, wkv_b)):
            for k in range(2):
                piece(kp(w_kv_up[l]).rearrange("p k (h two d) -> p k h two d", two=2, d=128)[:, k, :, two, :], 1024, hd, hd,
                      dstw[l][:, k * 1024:(k + 1) * 1024], ident)
                yield
        wmv = wm_b[l].rearrange("f p n -> p f n")
        for fc in range(16):
            for g in range(3):
                piece(kp(w_in[l])[:, :, 2624 + g * 2048 + fc * 128:2624 + g * 2048 + (fc + 1) * 128], 2048, kc, kc,
                      wm_b[l, fc][:, g * 2048:(g + 1) * 2048], ident)
                yield
        for f4 in range(4):
            piece(kp(w_conv_out[l])[:, :, f4 * 512:(f4 + 1) * 512], 2048, kfc(4), fkc(4), wmv[:, f4 * 4:(f4 + 1) * 4, 6144:6656], pfn(4))
            yield
            piece(kp(w_gqa_out[l])[:, :, f4 * 512:(f4 + 1) * 512], 2048, kfc(4), fkc(4), wmv[:, f4 * 4:(f4 + 1) * 4, 7680:8192], pfn(4))
            yield
        for f2 in range(8):
            piece(kp(w_mla_out[l])[:, :, f2 * 256:(f2 + 1) * 256], 2048, kfc(8), fkc(8), wmv[:, f2 * 2:(f2 + 1) * 2, 6656:7680], pfn(2))
            yield
        for fc in range(16):
            piece(kp(w_out[l])[:, :, fc * 128:(fc + 1) * 128], 2048, kc, kc, wo_b[l, fc], ident)
            yield
        for j in range(NFF):
            for two in range(2):
                piece(kp(w_ffn_in[l])[:, :, two * DFF + j * 128:two * DFF + (j + 1) * 128], 2048, kc, kc, wfi_b[l, 2 * j + two], ident)
                yield
        for half in range(2):
            for fc in range(16):
                for kk in range(2):
                    r0 = half * 2816 + kk * 1408
                    piece(kp(w_ffn_out[l][r0:r0 + 1408, :])[:, :, fc * 128:(fc + 1) * 128], 1408, kc, kc,
                          wfo_b[l, fc * 2 + half][:, kk * 1408:(kk + 1) * 1408], ident)
                    yield
        flush()

    ph = Phase(nc, "p0", persistent)
    ph.memset(ones_b[:], 1.0, [ones_b])
    ph.memset(ones_f[:], 1.0, [ones_f])
    stg_pool = ph.pool(3, [128, 8192], F32, "stg")
    cb_pool = ph.pool(3, [128, 8192], BF16, "cb")
    ci = [0]

    def cast(out_ap, in_ap, reads, writes):
        ci[0] += 1
        e = ("dve", "act", "pool")[ci[0] % 3]
        if e == "act":
            ph.act(out_ap, in_ap, AF.Copy, reads, writes)
        else:
            ph.op(e, lambda en: en.tensor_copy(out=out_ap, in_=in_ap), reads, writes)

    def stage_cast(src_ap, n_in, in_view, out_view, n_out, q="sp"):
        st = ph.nxt(stg_pool)
        ph.dma(q, in_view(st[:, :n_in]), src_ap, writes=[st])
        cb = ph.nxt(cb_pool)
        cast(out_view(cb[:, :n_out]), in_view(st[:, :n_in]), [st], [cb])
        return cb

    st = ph.nxt(stg_pool)
    ph.dma("sp", st[:64, :64], rmat, writes=[st])
    ph.op("dve", lambda en, st=st: en.tensor_copy(out=rm_b[:], in_=st[:64, :64]), [st], [rm_b])
    st = ph.nxt(stg_pool)
    ph.dma("sp", st[:, :1024].rearrange("p (a b) -> p a b", a=2), masks, writes=[st])
    ph.op("dve", lambda en, st=st: en.tensor_copy(out=mk_b[:], in_=st[:, :1024].rearrange("p (a b) -> p a b", a=2)), [st], [mk_b])
    ph.dma("sp", sm[:], smalls.rearrange("l p n -> p l n"), writes=[sm])
    cs = ph.tile([128, NK, 2], F32, "cs")
    ph.dma("sp", cs[:], cT, writes=[cs])
    ph.act(csb[:], cs[:], AF.Silu, [cs], [csb])
    for l in range(depth):
        ph.ts(gq_sc[:, l, :], sm[:, l, S_GQA:S_GQA + 4], float(np.sqrt(512.0)), None, ALU.mult, None, [sm], [gq_sc])
        ph.ts(gkv_sc[:, l, :], sm[:, l, S_GKV:S_GKV + 2], float(np.sqrt(256.0)), None, ALU.mult, None, [sm], [gkv_sc])
    run_all(gen_prep(ph, 0, ("dve", "act", "pool")))
    ph.finish()

    for l in range(depth):
        last = (l == depth - 1)
        xsrc = xin if l == 0 else x_s
        lsm = lambda c0, n=1: sm[:, l, c0:c0 + n]

        ph = Phase(nc, f"p1_{l}", persistent)
        wi = ph.tile([128, NK, 2624], BF16, "wi")
        wq = ph.tile([128, 4, 1536], BF16, "wq")
        wkk = ph.tile([128, 2, 1024], BF16, "wkk")
        wkv = ph.tile([128, 2, 1024], BF16, "wkv")
        wiv = wi_b[l].rearrange("p (k n) -> p k n", n=2624)
        for k in range(0, NK, 4):
            ph.dma("sp" if (k // 4) % 2 else "act", wi[:, k:k + 4, :], wiv[:, k:k + 4, :], writes=[wi])
        ph.dma("sp", wq[:], wq_b[l].rearrange("p (k n) -> p k n", n=1536), writes=[wq])
        ph.dma("sp", wkk[:], wkk_b[l].rearrange("p (k n) -> p k n", n=1024), writes=[wkk])
        ph.dma("act", wkv[:], wkv_b[l].rearrange("p (k n) -> p k n", n=1024), writes=[wkv])
        xc_pool = ph.pool(4, [128, 512], F32, "xc")
        sq_pool = ph.pool(2, [128, 512], BF16, "sq")
        rstd_pool = ph.pool(2, [128, 512], F32, "rstd")
        hT_pool = ph.pool(2, [128, NK, 512], BF16, "hT")
        f32_pool = ph.pool(4, [128, 512], F32, "f32")
        b16_pool = ph.pool(4, [128, 512], BF16, "b16")
        n32_pool = ph.pool(2, [128, 4, 512], F32, "n32")
        nb_pool = ph.pool(2, [128, 4, 512], BF16, "nb")
        vb_pool = ph.pool(2, [128, 1024], BF16, "vb")
        cs_pool = ph.pool(2, [64, 2, 512], F32, "cs")
        xsv = xsrc.rearrange("(k p) t -> p k t", p=128)
        hsv = h_s.rearrange("(k p) t -> p k t", p=128)

        def rms_stats(srcs, n, T, eps_n):
            pt = ps()
            for i, src in enumerate(srcs):
                ap, tl = src() if callable(src) else src
                sq = ph.nxt(sq_pool)
                ph.act(sq[:, :T], ap, AF.Square, [tl], [sq])
                ph.mm(pt, pt[:, :T], ones_b[:], sq[:, :T], i == 0, i == len(srcs) - 1, reads=[sq, ones_b], inc=True)
            r = ph.nxt(rstd_pool)
            ph.rsq(r[:, :T], pt[:, :T], eps_n, [pt], [r])
            return r

        def rope_store(pt, np_, T, t0, is_ctx, dst_ap, cst):
            xb = ph.nxt(b16_pool)
            ph.act(xb[:np_, :T], pt[:np_, :T], AF.Copy, [pt], [xb])
            if is_ctx:
                ph.dma("sp", dst_ap, xb[:np_, :T], reads=[xb])
                return
            p2 = ps()
            ph.mm(p2, p2[:64, :T], rm_b[:], xb[:64, :T], True, True, reads=[xb, rm_b])
            t1 = ph.nxt(f32_pool)
            ph.tt(t1[:64, :T], pt[:64, :T], cst[:, 0, :T], ALU.mult, [pt, cst, xb], [t1])
            t2 = ph.nxt(f32_pool)
            ph.tt(t2[:64, :T], p2[:64, :T], cst[:, 1, :T], ALU.mult, [p2, cst], [t2])
            ob = ph.nxt(b16_pool)
            ph.tt(ob[:64, :T], t1[:64, :T], t2[:64, :T], ALU.add, [t1, t2], [ob], eng="dve" if os.environ.get("MK_T2") else "pool")
            ph.dma("sp", dst_ap, ob[:64, :T], reads=[ob])

        def norm_tile(t0, T, is_ctx):
            col = 1 if is_ctx else 0
            def ldx(k, t0=t0, T=T):
                def f():
                    xc = ph.nxt(xc_pool)
                    ph.dma("sp", xc[:, :T], xsv[:, k, t0:t0 + T], writes=[xc])
                    return xc[:, :T], xc
                return f
            rstd = rms_stats([ldx(k) for k in range(NK)], D, T, D * EPS)
            hT = ph.nxt(hT_pool)
            for k in range(NK):
                xc = ph.nxt(xc_pool)
                ph.dma("sp", xc[:, :T], xsv[:, k, t0:t0 + T], writes=[xc])
                ph.stt(xc[:, :T], xc[:, :T], A1[l][:, k, col:col + 1], rstd[:, :T], ALU.mult, ALU.mult, [xc, A1[l], rstd], [xc])
                ph.act(hT[:, k, :T], xc[:, :T], AF.Identity, [xc, mod[l]], [hT], bias=mod[l][:, k, col:col + 1])
            ph.dma("sp", hsv[:, :, t0:t0 + T], hT[:, :, :T], reads=[hT])
            if not is_ctx:
                cst = ph.nxt(cs_pool)
                ph.dma("sp", cst[:, 0, :T], ropec[:, t0:t0 + T], writes=[cst])
                ph.dma("sp", cst[:, 1, :T], ropes[:, t0:t0 + T], writes=[cst])
            else:
                cst = None
            return hT, cst

        nxt_norm = norm_tile(*tiles[0])
        for ti, (t0, T, is_ctx) in enumerate(tiles):
            col = 1 if is_ctx else 0
            skip_q = is_ctx and last
            hT, cst = nxt_norm
            if ti + 1 < len(tiles):
                nxt_norm = norm_tile(*tiles[ti + 1])

            def proj(c0, m):
                pt = ps()
                for k in range(NK):
                    ph.mm(pt, pt[:m, :T], wi[:, k, c0:c0 + m], hT[:, k, :T], k == 0, k == NK - 1, reads=[wi, hT])
                return pt

            c32 = ph.nxt(n32_pool)
            for c in range(2):
                pt = proj(c * 128, 128)
                ph.act(c32[:, c, :T], pt[:, :T], AF.Copy, [pt], [c32])
            r = rms_stats([(c32[:, c, :T], c32) for c in range(2)], 256, T, 256 * EPS)
            cn = ph.nxt(nb_pool)
            for c in range(2):
                ph.stt(cn[:, c, :T], c32[:, c, :T], gkv_sc[:, l, c:c + 1], r[:, :T], ALU.mult, ALU.mult, [c32, gkv_sc, r], [cn])
            if not skip_q:
                q32 = ph.nxt(n32_pool)
                for c in range(4):
                    pt = proj(576 + c * 128, 128)
                    ph.act(q32[:, c, :T], pt[:, :T], AF.Copy, [pt], [q32])
                r = rms_stats([(q32[:, c, :T], q32) for c in range(4)], 512, T, 512 * EPS)
                qn = ph.nxt(nb_pool)
                for c in range(4):
                    ph.stt(qn[:, c, :T], q32[:, c, :T], gq_sc[:, l, c:c + 1], r[:, :T], ALU.mult, ALU.mult, [q32, gq_sc, r], [qn])
            pt = proj(256, 64)
            rope_store(pt, 64, T, t0, is_ctx, kr_s[:, t0:t0 + T], cst)
            for g in range(2):
                pt = proj(256 if os.environ.get("MK_T1") else 320 + g * 64, 64)
                rope_store(pt, 64, T, t0, is_ctx, gk_s[g, :, t0:t0 + T], cst)
            pt = ps()
            for tc in range(T // 128):
                for k in range(NK):
                    ph.mm(pt, pt[:, tc * 128:(tc + 1) * 128], hT[:, k, tc * 128:(tc + 1) * 128], wi[:, k, 448:576], k == 0, k == NK - 1,
                          reads=[wi, hT], inc=(k == NK - 1 and tc == T // 128 - 1))
            ob = ph.nxt(b16_pool)
            ph.act(ob[:, :T], pt[:, :T], AF.Copy, [pt], [ob])
            ph.dma("sp", gv_s[t0:t0 + T, :].rearrange("(c p) d -> p c d", p=128), ob[:, :T].rearrange("p (c d) -> p c d", d=128), reads=[ob])
            if not skip_q:
                for h in range(8):
                    pt = proj(1088 + h * 64, 64)
                    rope_store(pt, 64, T, t0, is_ctx, gq_s[h, :, t0:t0 + T], cst)
                for c in range(4):
                    pb = proj(2112 + c * 128, 128)
                    sg = ph.nxt(f32_pool)
                    ph.act(sg[:, :T], pb[:, :T], AF.Sigmoid, [pb], [sg])
                    pa = proj(1600 + c * 128, 128)
                    ob = ph.nxt(b16_pool)
                    ph.tt(ob[:, :T], pa[:, :T], sg[:, :T], ALU.mult, [pa, sg], [ob])
                    ph.dma("sp", u_s[c * 128:(c + 1) * 128, t0:t0 + T], ob[:, :T], reads=[ob])
            for h in range(8):
                pt = ps()
                for k in range(2):
                    ph.mm(pt, pt[:, :T], wkk[:, k, h * 128:(h + 1) * 128], cn[:, k, :T], k == 0, k == 1, reads=[wkk, cn])
                ob = ph.nxt(b16_pool)
                ph.act(ob[:, :T], pt[:, :T], AF.Copy, [pt], [ob])
                ph.dma("sp", kn_s[h, :, t0:t0 + T], ob[:, :T], reads=[ob])
            for tc in range(T // 128):
                vb = ph.nxt(vb_pool)
                for hv in range(2):
                    pt = ps()
                    for k in range(2):
                        ph.mm(pt, pt[:, :], cn[:, k, tc * 128:(tc + 1) * 128], wkv[:, k, hv * 512:(hv + 1) * 512], k == 0, k == 1,
                              reads=[wkv, cn])
                    ph.act(vb[:, hv * 512:(hv + 1) * 512], pt[:, :], AF.Copy, [pt], [vb])
                ph.dma("sp", mv_s[t0 + tc * 128:t0 + (tc + 1) * 128, :], vb[:], reads=[vb])
            if not skip_q:
                for h in range(8):
                    pt = ps()
                    for k in range(4):
                        ph.mm(pt, pt[:, :T], wq[:, k, h * 192:h * 192 + 128], qn[:, k, :T], k == 0, k == 3, reads=[wq, qn])
                    ob = ph.nxt(b16_pool)
                    ph.act(ob[:, :T], pt[:, :T], AF.Copy, [pt], [ob])
                    ph.dma("sp", qn_s[h, :, t0:t0 + T], ob[:, :T], reads=[ob])
                    pt = ps()
                    for k in range(4):
                        ph.mm(pt, pt[:64, :T], wq[:, k, h * 192 + 128:h * 192 + 192], qn[:, k, :T], k == 0, k == 3, reads=[wq, qn])
                    rope_store(pt, 64, T, t0, is_ctx, qr_s[h, :, t0:t0 + T], cst)
        ph.finish()

        ph = Phase(nc, f"p2x_{l}", persistent)
        nkc = Lt // 128
        p_pool = ph.pool(6, [128, 512], BF16, "p")
        cvv = cv_s.rearrange("(c p) t -> p c t", p=128)

        def gen_mla():
            kr = ph.tile([64, Lt], BF16, "kr")
            ph.dma("sp", kr[:], kr_s, writes=[kr])
            kn_pool = ph.pool(2, [128, Lt], BF16, "kn")
            v_pool = ph.pool(2, [128, nkc, 128], BF16, "v")
            qn_pool = ph.pool(2, [128, Lt], BF16, "qn")
            qr_pool = ph.pool(2, [64, Lt], BF16, "qr")
            rc_pool = ph.pool(2, [128, 512], F32, "rc")
            o_pool = ph.pool(2, [128, 512], BF16, "o")
            sc_mla = float(192.0 ** -0.5)
            yield
            for h in range(8):
                kn = ph.nxt(kn_pool)
                v = ph.nxt(v_pool)
                qn = ph.nxt(qn_pool)
                qr = ph.nxt(qr_pool)
                ph.dma("sp", kn[:], kn_s[h], writes=[kn])
                ph.dma("sp", v[:], mv_s[:, h * 128:(h + 1) * 128].rearrange("(c p) d -> p c d", p=128), writes=[v])
                ph.dma("sp", qn[:], qn_s[h], writes=[qn])
                ph.dma("sp", qr[:], qr_s[h], writes=[qr])
                for (t0, T, is_ctx) in tiles:
                    if is_ctx and last:
                        continue
                    chunks = [nkc - 2, nkc - 1] if is_ctx else list(range(nkc))
                    po = psa()
                    pd = psa()
                    LA = 2
                    pend = []
                    for i, kc in enumerate(chunks + [None] * LA):
                        if kc is not None:
                            pst = pss()
                            ph.mm(pst, pst[:, :T], kn[:, kc * 128:(kc + 1) * 128], qn[:, t0:t0 + T], True, False, reads=[kn, qn], inc=False)
                            ph.mm(pst, pst[:, :T], kr[:, kc * 128:(kc + 1) * 128], qr[:, t0:t0 + T], False, True, reads=[kr, qr], inc=True)
                            p = ph.nxt(p_pool)
                            ph.act(p[:, :T], pst[:, :T], AF.Exp, [pst], [p], scale=sc_mla)
                            pend.append((i, kc, p))
                        if i >= LA:
                            j, kcj, pj = pend.pop(0)
                            fst, lst = j == 0, j == len(chunks) - 1
                            ph.mm(po, po[:, :T], v[:, kcj, :], pj[:, :T], fst, lst, reads=[v, pj], inc=lst)
                            ph.mm(pd, pd[:, :T], ones_b[:], pj[:, :T], fst, lst, reads=[pj, ones_b], inc=True)
                    rc = ph.nxt(rc_pool)
                    ph.act(rc[:, :T], pd[:, :T], AF.Ln, [pd], [rc])
                    ph.act(rc[:, :T], rc[:, :T], AF.Exp, [rc], [rc], scale=-1.0)
                    o = ph.nxt(o_pool)
                    ph.tt(o[:, :T], po[:, :T], rc[:, :T], ALU.mult, [po, rc], [o])
                    ph.dma("sp", om_s[h * 128:(h + 1) * 128, t0:t0 + T], o[:, :T], reads=[o])
                    yield

        def gen_gqa():
            gk = ph.tile([64, 2, Lt], BF16, "gk")
            gv = ph.tile([128, nkc, 128], BF16, "gv")
            ph.dma("sp", gk[:], gk_s.rearrange("g d t -> d g t"), writes=[gk])
            ph.dma("sp", gv[:], gv_s.rearrange("(c p) d -> p c d", p=128), writes=[gv])
            sinkx = ph.tile([64, 2, 512], F32, "sinkx")
            for g in range(2):
                for hh in range(4):
                    ph.act(sinkx[:, g, hh * 128:(hh + 1) * 128],
                           sm[0:64, l, S_SINK + 4 * g + hh:S_SINK + 4 * g + hh + 1].to_broadcast([64, 128]), AF.Exp, [sm], [sinkx])
            gq_pool = ph.pool(2, [64, 8, 512], BF16, "gq")
            ds_pool = ph.pool(2, [64, 512], F32, "ds")
            o_pool = ph.pool(2, [64, 512], BF16, "o")
            sc_gqa = float(64.0 ** -0.5)
            nlb = L // 128
            ogv = og_s.rearrange("(h d) t -> d h t", d=64)
            yield
            gp_pool = ph.pool(12, [128, 512], BF16, "gp")
            prevB = None

            def stageB(st):
                (t0, qb, g, plist) = st
                po = psa()
                pd = psa()
                for j, (kcj, pj) in enumerate(plist):
                    fst, lst = j == 0, j == len(plist) - 1
                    ph.mm(po, po[:64, :], gv[:, kcj, g * 64:(g + 1) * 64], pj[:], fst, lst, reads=[gv, pj], inc=lst)
                    ph.mm(pd, pd[:64, :], ones_b[:, 0:64], pj[:], fst, lst, reads=[pj, ones_b], inc=True)
                ds = ph.nxt(ds_pool)
                ph.tt(ds[:], pd[:64, :], sinkx[:, g, :], ALU.add, [pd, sinkx], [ds])
                ph.act(ds[:], ds[:], AF.Ln, [ds], [ds])
                ph.act(ds[:], ds[:], AF.Exp, [ds], [ds], scale=-1.0)
                o = ph.nxt(o_pool)
                ph.tt(o[:], po[:64, :], ds[:], ALU.mult, [po, ds], [o])
                ph.dma("sp", ogv[:, 4 * g:4 * g + 4, t0 + qb * 128:t0 + (qb + 1) * 128], o[:].rearrange("d (h t) -> d h t", t=128), reads=[o])

            for (t0, T, is_ctx) in tiles:
                if is_ctx and last:
                    continue
                gq = ph.nxt(gq_pool)
                ph.dma("sp", gq[:, :, :T], gq_s[:, :, t0:t0 + T].rearrange("h d t -> d h t"), writes=[gq])
                for qb in range(T // 128):
                    blk = (t0 + qb * 128) // 128
                    if is_ctx:
                        chunks = [(nkc - 2, None), (nkc - 1, None)]
                    else:
                        chunks = []
                        if blk > 0:
                            chunks.append((blk - 1, 0))
                        chunks.append((blk, None))
                        if blk < nlb - 1:
                            chunks.append((blk + 1, 1))
                        chunks += [(nkc - 2, None), (nkc - 1, None)]
                    for g in range(2):
                        if prevB is not None:
                            stageB(prevB)
                        plist = []
                        for (kc, mi) in chunks:
                            pst = pss()
                            ph.mm(pst, pst[:, :].rearrange("p (h t) -> p h t", t=128), gk[:, g, kc * 128:(kc + 1) * 128], gq[:, 4 * g:4 * g + 4, qb * 128:(qb + 1) * 128], True, True,
                                  reads=[gk, gq])
                            p = ph.nxt(gp_pool)
                            ph.act(p[:], pst[:], AF.Exp, [pst], [p], scale=sc_gqa)
                            if mi is not None:
                                ph.tt(p[:], p[:], mk_b[:, mi, :], ALU.mult, [p, mk_b], [p], eng="pool")
                            plist.append((kc, p))
                        prevB = (t0, qb, g, plist)
                        yield
            if prevB is not None:
                stageB(prevB)
            yield

        def gen_conv():
            u_pool = ph.pool(2, [128, 4, 512 + 30], BF16, "u")
            acc_pool = [ph.pool(2, [128, 512], F32, f"acc{c}") for c in range(4)]
            sq_pool = ph.pool(2, [128, 512], F32, "sq")
            st_pool = ph.pool(1, [128, 3, 512], F32, "st")
            cvo_pool = ph.pool(2, [128, 4, 512], BF16, "cvo")
            usv = u_s.rearrange("(c p) t -> p c t", p=128)
            cvv = cv_s.rearrange("(c p) t -> p c t", p=128)
            yield
            for (t0, T, is_ctx) in tiles:
                if is_ctx and last:
                    continue
                s0, s1 = (L, Lt) if is_ctx else (0, L)
                u = ph.nxt(u_pool)
                lo, hi = max(s0, t0 - 15), min(s1, t0 + T + 15)
                if lo > t0 - 15:
                    ph.memset(u[:, :, 0:15], 0.0, [u], eng="dve")
                if hi < t0 + T + 15:
                    ph.memset(u[:, :, T + 15:T + 30], 0.0, [u], eng="dve")
                ph.dma("sp", u[:, :, lo - (t0 - 15):hi - (t0 - 15)], usv[:, :, lo:hi], writes=[u])
                accs = [ph.nxt(acc_pool[c]) for c in range(4)]
                eng = "dve"
                for j in range(31):
                    for c in range(4):
                        acc = accs[c]
                        if j == 0:
                            ph.ts(acc[:, :T], u[:, c, 0:T], lsm(S_CW + c * 31), lsm(S_CB + c), ALU.mult, ALU.add, [u, sm], [acc], eng=eng)
                        else:
                            ph.stt(acc[:, :T], u[:, c, j:j + T], lsm(S_CW + c * 31 + j), acc[:, :T], ALU.mult, ALU.add, [u, sm, acc], [acc], eng=eng)
                    if j % 2 == 1:
                        yield
                yield
                p1 = ps()
                p2 = ps()
                for c in range(4):
                    ph.mm(p1, p1[:, :T], ones_f[:], accs[c][:, :T], c == 0, c == 3, reads=[accs[c], ones_f], inc=True)
                for c in range(4):
                    sq = ph.nxt(sq_pool)
                    ph.act(sq[:, :T], accs[c][:, :T], AF.Square, [accs[c]], [sq])
                    ph.mm(p2, p2[:, :T], ones_f[:], sq[:, :T], c == 0, c == 3, reads=[sq, ones_f], inc=True)
                st = ph.nxt(st_pool)
                ph.ts(st[:, 0, :T], p1[:, :T], 1.0 / 512, None, ALU.mult, None, [p1], [st])
                ph.tt(st[:, 1, :T], st[:, 0, :T], st[:, 0, :T], ALU.mult, [st], [st])
                ph.stt(st[:, 2, :T], p2[:, :T], 1.0 / 512, st[:, 1, :T], ALU.mult, ALU.subtract, [p2, st], [st])
                ph.rsq(st[:, 2, :T], st[:, 2, :T], EPS, [st], [st])
                cvo = ph.nxt(cvo_pool)
                for c in range(4):
                    ph.tt(accs[c][:, :T], accs[c][:, :T], st[:, 0, :T], ALU.subtract, [accs[c], st], [accs[c]])
                for c in range(4):
                    ph.tt(accs[c][:, :T], accs[c][:, :T], st[:, 2, :T], ALU.mult, [accs[c], st], [accs[c]])
                for c in range(4):
                    ph.act(cvo[:, c, :T], accs[c][:, :T], AF.Silu, [accs[c], sm], [cvo], bias=lsm(S_LNB + c), scale=lsm(S_LNG + c))
                ph.dma("sp", cvv[:, :, t0:t0 + T], cvo[:, :, :T], reads=[cvo])
                yield

        gens = [gen_mla(), gen_gqa(), gen_conv()]
        for g_ in gens:
            next(g_)
        alive = [True, True, True]
        step = 0
        while any(alive):
            step += 1
            order = [0, 1, 2, 2]
            if not alive[0]:
                order = [1, 2, 2, 2]
            for gi in order:
                if alive[gi]:
                    try:
                        next(gens[gi])
                    except StopIteration:
                        alive[gi] = False
        ph.finish()

        ph = Phase(nc, f"p2c_{l}", persistent)
        xt_pool = ph.pool(1, [128, NK, 512], F32, "xt")
        ht_pool = ph.pool(1, [128, NK, 512], BF16, "ht")
        yT_pool = ph.pool(1, [128, NK, 512], BF16, "yT")
        at_pool = ph.pool(1, [128, 22, 512], BF16, "at")
        cv_pool = ph.pool(1, [128, 4, 512], BF16, "cv")
        om_pool = ph.pool(1, [128, 8, 512], BF16, "om")
        og_pool = ph.pool(1, [128, 4, 512], BF16, "og")
        wb_pool = ph.pool(3, [128, 8192], BF16, "wb")
        sg_pool = ph.pool(3, [128, 512], F32, "sg")
        t_pool = ph.pool(3, [128, 512], F32, "tt")
        acc_pool = ph.pool(2, [128, 512], F32, "acc")
        sq_pool = ph.pool(3, [128, 512], BF16, "sq")
        rstd_pool = ph.pool(1, [128, 512], F32, "rstd")
        xdv = x_s.rearrange("(k p) t -> p k t", p=128)
        omv = om_s.rearrange("(k p) t -> p k t", p=128)
        ogv2 = og_s.rearrange("(k p) t -> p k t", p=128)
        wq_i = [0]

        prep = None
        if not last:
            prep = gen_prep(ph, l + 1, ("pool",))
            next(prep)

        def prep_step(nsteps=1):
            nonlocal prep
            for _ in range(nsteps):
                if prep is not None:
                    try:
                        next(prep)
                    except StopIteration:
                        prep = None

        def wload(src_ap, n):
            wb = ph.nxt(wb_pool)
            wq_i[0] += 1
            ph.dma("sp", wb[:, :n], src_ap, writes=[wb])
            if wq_i[0] % 3 != 0:
                prep_step()
            return wb

        xt = ph.nxt(xt_pool)
        ht = ph.nxt(ht_pool)
        yT = ph.nxt(yT_pool)
        at = ph.nxt(at_pool)
        cv = ph.nxt(cv_pool)
        om = ph.nxt(om_pool)
        og = ph.nxt(og_pool)
        c_tiles = [t for t in tiles if not (t[2] and last)]

        def load_small(t0, T):
            ph.dma("sp", ht[:, :, :T], hsv[:, :, t0:t0 + T], writes=[ht])
            ph.dma("sp", cv[:, :, :T], cvv[:, :, t0:t0 + T], writes=[cv])
            ph.dma("sp", om[:, :, :T], omv[:, :, t0:t0 + T], writes=[om])
            ph.dma("sp", og[:, :, :T], ogv2[:, :, t0:t0 + T], writes=[og])

        def load_x(t0, T):
            for k in range(0, NK, 4):
                ph.dma("sp", xt[:, k:k + 4, :T], xsv[:, k:k + 4, t0:t0 + T], writes=[xt])

        load_small(*c_tiles[0][:2])
        load_x(*c_tiles[0][:2])
        for ti, (t0, T, is_ctx) in enumerate(c_tiles):
            col = 1 if is_ctx else 0
            for fc in range(16):
                wb = wload(wm_b[l, fc], 8192)
                acc = ph.nxt(acc_pool)
                for g, (src, nk, off) in enumerate(((cv, 4, 6144), (om, 8, 6656), (og, 4, 7680))):
                    pg = ps()
                    for k in range(NK):
                        ph.mm(pg, pg[:, :T], wb[:, g * 2048 + k * 128:g * 2048 + (k + 1) * 128], ht[:, k, :T], k == 0, k == NK - 1, reads=[wb, ht])
                    sg = ph.nxt(sg_pool)
                    ph.act(sg[:, :T], pg[:, :T], AF.Sigmoid, [pg], [sg])
                    py = ps()
                    for k in range(nk):
                        ph.mm(py, py[:, :T], wb[:, off + k * 128:off + (k + 1) * 128], src[:, k, :T], k == 0, k == nk - 1, reads=[wb, src])
                    if g == 0:
                        ph.tt(acc[:, :T], py[:, :T], sg[:, :T], ALU.mult, [py, sg], [acc])
                    else:
                        tq = ph.nxt(t_pool)
                        ph.tt(tq[:, :T], py[:, :T], sg[:, :T], ALU.mult, [py, sg], [tq])
                        if g == 1:
                            ph.tt(acc[:, :T], acc[:, :T], tq[:, :T], ALU.add, [acc, tq], [acc], eng="pool")
                        else:
                            ph.tt(yT[:, fc, :T], acc[:, :T], tq[:, :T], ALU.add, [acc, tq], [yT], eng="pool")
            for f4 in range(4):
                wb = wload(wo_b[l, f4 * 4:(f4 + 1) * 4].rearrange("f p n -> p f n"), 8192)
                for fi in range(4):
                    fc = f4 * 4 + fi
                    po = ps()
                    for k in range(NK):
                        ph.mm(po, po[:, :T], wb[:, fi * 2048 + k * 128:fi * 2048 + (k + 1) * 128], yT[:, k, :T], k == 0, k == NK - 1, reads=[wb, yT])
                    ph.stt(xt[:, fc, :T], po[:, :T], mod[l][:, 2 * 16 + fc, col:col + 1], xt[:, fc, :T], ALU.mult, ALU.add, [po, mod[l], xt], [xt])
            pt = ps()
            for k in range(NK):
                sq = ph.nxt(sq_pool)
                ph.act(sq[:, :T], xt[:, k, :T], AF.Square, [xt], [sq])
                ph.mm(pt, pt[:, :T], ones_b[:], sq[:, :T], k == 0, k == NK - 1, reads=[sq, ones_b], inc=True)
            rstd = ph.nxt(rstd_pool)
            ph.rsq(rstd[:, :T], pt[:, :T], D * EPS, [pt], [rstd])
            for k in range(NK):
                tq = ph.nxt(t_pool)
                ph.stt(tq[:, :T], xt[:, k, :T], A2[l][:, k, col:col + 1], rstd[:, :T], ALU.mult, ALU.mult, [xt, A2[l], rstd], [tq])
                ph.act(ht[:, k, :T], tq[:, :T], AF.Identity, [tq, mod[l]], [ht], bias=mod[l][:, 3 * 16 + k, col:col + 1])
            for half in range(2):
                for j2 in range(0, 22, 2):
                    j = half * 22 + j2
                    wb = wload(wfi_b[l, 2 * j:2 * j + 4].rearrange("f p n -> p f n"), 8192)
                    for jj in range(2):
                        pa = ps()
                        for k in range(NK):
                            ph.mm(pa, pa[:, :T], wb[:, (2 * jj) * 2048 + k * 128:(2 * jj) * 2048 + (k + 1) * 128], ht[:, k, :T], k == 0, k == NK - 1, reads=[wb, ht])
                        sg = ph.nxt(sg_pool)
                        ph.act(sg[:, :T], pa[:, :T], AF.Silu, [pa], [sg])
                        pb = ps()
                        for k in range(NK):
                            ph.mm(pb, pb[:, :T], wb[:, (2 * jj + 1) * 2048 + k * 128:(2 * jj + 1) * 2048 + (k + 1) * 128], ht[:, k, :T], k == 0, k == NK - 1, reads=[wb, ht])
                        ph.tt(at[:, j2 + jj, :T], pb[:, :T], sg[:, :T], ALU.mult, [pb, sg], [at])
                for fc in range(16):
                    wb = wload(wfo_b[l, fc * 2 + half], 22 * 128)
                    po = ps()
                    for k in range(22):
                        ph.mm(po, po[:, :T], wb[:, k * 128:(k + 1) * 128], at[:, k, :T], k == 0, k == 21, reads=[wb, at])
                    ph.stt(xt[:, fc, :T], po[:, :T], mod[l][:, 5 * 16 + fc, col:col + 1], xt[:, fc, :T], ALU.mult, ALU.add, [po, mod[l], xt], [xt])
            if ti + 1 < len(c_tiles):
                load_small(*c_tiles[ti + 1][:2])
            for k in range(0, NK, 4):
                ph.dma("sp", xdv[:, k:k + 4, t0:t0 + T], xt[:, k:k + 4, :T], reads=[xt])
            if ti + 1 < len(c_tiles):
                load_x(*c_tiles[ti + 1][:2])
        while prep is not None:
            prep_step()
        ph.finish()

    ph = Phase(nc, "fin", persistent)
    xt_pool = ph.pool(2, [128, NK, 512], F32, "xt")
    sq_pool = ph.pool(3, [128, 512], BF16, "sq")
    rstd_pool = ph.pool(2, [128, 512], F32, "rstd")
    gf = ph.tile([128, NK], F32, "gf")
    ph.ts(gf[:], sm[:, 0, S_GFIN:S_GFIN + 16], float(np.sqrt(D)), None, ALU.mult, None, [sm], [gf])
    xdv = x_s.rearrange("(k p) t -> p k t", p=128)
    odv = outT.rearrange("(k p) t -> p k t", p=128)
    for (t0, T, is_ctx) in tiles:
        if is_ctx:
            continue
        xt = ph.nxt(xt_pool)
        for k in range(0, NK, 4):
            ph.dma("sp", xt[:, k:k + 4, :T], xdv[:, k:k + 4, t0:t0 + T], writes=[xt])
        pt = ps()
        for k in range(NK):
            sq = ph.nxt(sq_pool)
            ph.act(sq[:, :T], xt[:, k, :T], AF.Square, [xt], [sq])
            ph.mm(pt, pt[:, :T], ones_b[:], sq[:, :T], k == 0, k == NK - 1, reads=[sq, ones_b], inc=True)
        rstd = ph.nxt(rstd_pool)
        ph.rsq(rstd[:, :T], pt[:, :T], D * EPS, [pt], [rstd])
        for k in range(NK):
            ph.stt(xt[:, k, :T], xt[:, k, :T], gf[:, k:k + 1], rstd[:, :T], ALU.mult, ALU.mult, [xt, gf, rstd], [xt])
        for k in range(0, NK, 4):
            ph.dma("sp", odv[:, k:k + 4, t0:t0 + T], xt[:, k:k + 4, :T], reads=[xt])
    ph.finish()
    glob.close()
    return nc


def host_consts(L):
    GRID_W = 64
    rows = L // GRID_W
    row = np.repeat(np.arange(rows), GRID_W).astype(np.float32)
    colp = np.tile(np.arange(GRID_W), rows).astype(np.float32)
    inv_freq = (np.float32(10000.0) ** (-np.arange(16, dtype=np.float32) / np.float32(16))).astype(np.float32)
    a_row = row[:, None] * inv_freq[None, :]
    a_col = colp[:, None] * inv_freq[None, :]
    ang = np.concatenate([a_row, a_row, a_col, a_col], axis=-1).astype(np.float32)
    ropec = np.ascontiguousarray(np.cos(ang).T.astype(np.float32))
    ropes = np.ascontiguousarray(np.sin(ang).T.astype(np.float32))
    rmat = np.zeros((64, 64), np.float32)
    for m in range(64):
        if m % 32 < 16:
            rmat[m + 16, m] = -1.0
        else:
            rmat[m - 16, m] = 1.0
    j = np.arange(128)[:, None]
    i = np.arange(128)[None, :]
    mA = (j >= i).astype(np.float32)
    mB = (j <= i).astype(np.float32)
    masks = np.stack([np.tile(mA, (1, 4)), np.tile(mB, (1, 4))], axis=1)
    return ropec, ropes, rmat, np.ascontiguousarray(masks)


def pack_smalls(depth, b_ada, g_mix, g_ffn, g_q_a, g_kv_a, conv_b, conv_ln_g, conv_ln_b, conv_w, gqa_sink, g_final):
    sm = np.zeros((depth, 128, NS), np.float32)

    def fm(v):
        return np.asarray(v, np.float32).reshape(-1, 128).T

    for l in range(depth):
        sm[l, :, S_GMIX:S_GMIX + 16] = fm(g_mix[l])
        sm[l, :, S_GFFN:S_GFFN + 16] = fm(g_ffn[l])
        sm[l, :, S_BADA:S_BADA + 96] = fm(b_ada[l])
        sm[l, :, S_GQA:S_GQA + 4] = fm(g_q_a[l])
        sm[l, :, S_GKV:S_GKV + 2] = fm(g_kv_a[l])
        sm[l, :, S_CB:S_CB + 4] = fm(conv_b[l])
        sm[l, :, S_LNG:S_LNG + 4] = fm(conv_ln_g[l])
        sm[l, :, S_LNB:S_LNB + 4] = fm(conv_ln_b[l])
        cw = np.asarray(conv_w[l], np.float32)
        for c in range(4):
            sm[l, :, S_CW + c * 31:S_CW + (c + 1) * 31] = cw[:, c * 128:(c + 1) * 128].T
        sm[l, :, S_SINK:S_SINK + 8] = np.asarray(gqa_sink[l], np.float32)[None, :]
        sm[l, :, S_GFIN:S_GFIN + 16] = fm(g_final)
    return sm


_CACHE = {}


def run(inputs, L, depth, n_cores, dbg=()):
    key = (L, depth, tuple(dbg))
    if key not in _CACHE:
        _CACHE[key] = build(L, depth, dbg)
    nc = _CACHE[key]
    f = lambda a: np.ascontiguousarray(np.asarray(a, np.float32))
    x, c, ctx, c_ctx = f(inputs["x"]), f(inputs["c"]), f(inputs["ctx"]), f(inputs["c_ctx"])
    ropec, ropes, rmat, masks = host_consts(L)
    sm = pack_smalls(depth, *[np.asarray(inputs[k]) for k in ("b_ada", "g_mix", "g_ffn", "g_q_a", "g_kv_a", "conv_b", "conv_ln_g",
                                                               "conv_ln_b", "conv_w", "gqa_sink")], np.asarray(inputs["g_final"]))
    shared = {"smalls": sm, "ropec": ropec, "ropes": ropes, "rmat": rmat, "masks": masks}
    for k in ("w_ada", "w_in", "w_conv_out", "w_q_up", "w_kv_up", "w_mla_out", "w_gqa_out", "w_out", "w_ffn_in", "w_ffn_out"):
        shared[k] = f(inputs[k])[:depth]
    in_maps = []
    for b in range(n_cores):
        m = dict(shared)
        m["xin"] = np.ascontiguousarray(np.concatenate([x[b].T, ctx[b].T], axis=1))
        cc = np.stack([c[b], c_ctx], axis=1)
        m["cT"] = np.ascontiguousarray(cc.reshape(NK, 128, 2).transpose(1, 0, 2))
        in_maps.append(m)
    res = run_bass_kernel_spmd(nc, in_maps, core_ids=list(range(n_cores)))
    return res


def kernel(**inputs):
    B, L, _ = inputs["x"].shape
    res = run(inputs, L, 4, B)
    out = np.stack([np.ascontiguousarray(r["outT"].T) for r in res.results], axis=0)
    return out.astype(np.float32)
```

```python
import numpy as np
import os
from contextlib import ExitStack
import concourse.bass as bass
import concourse.mybir as mybir
from concourse.bass_utils import run_bass_kernel_spmd

F32 = mybir.dt.float32
BF16 = mybir.dt.bfloat16
AF = mybir.ActivationFunctionType
ALU = mybir.AluOpType

D = 2048
NK = 16
CTX = 256
DFF = 5632
NFF = 44
INC = 8768
EPS = 1e-6
ENGS = ["pe", "act", "dve", "pool", "sp"]
SAME_ENGINE_SYNC = True

S_GMIX, S_GFFN, S_BADA, S_GQA, S_GKV, S_CB, S_LNG, S_LNB, S_CW, S_SINK, S_GFIN = 0, 16, 32, 128, 132, 134, 138, 142, 146, 270, 278
NS = 294


class Buf:
    __slots__ = ("name", "w", "r", "dsem", "dcnt")

    def __init__(self, name):
        self.name = name
        self.w = None
        self.r = {}
        self.dsem = None
        self.dcnt = 0


class Tile:
    def __init__(self, t, name):
        self.t = t
        self.b = Buf(name)

    def __getitem__(self, k):
        return self.t[k]


class Phase:
    def __init__(self, nc, name, persistent=()):
        self.nc = nc
        self.name = name
        self.es = ExitStack()
        self.ops = {e: [] for e in ENGS}
        self.sem = {e: nc.alloc_semaphore(name=f"{name}_{e}") for e in ENGS}
        self.allsems = list(self.sem.values())
        self.cnt = {e: 0 for e in ENGS}
        self.seen = {e: {} for e in ENGS}
        self.dbufs = []
        self.bufs = [t.b for t in persistent]
        self.n = 0
        self.rr = {}
        self.pending = {}
        self.seq = 0

    def tile(self, shape, dt, name=None):
        self.n += 1
        name = f"{self.name}_{name or 't'}{self.n}"
        t = Tile(self.es.enter_context(self.nc.sbuf_tensor(name, list(shape), dt)), name)
        self.bufs.append(t.b)
        return t

    def pool(self, n, shape, dt, name):
        return [self.tile(shape, dt, f"{name}{i}_") for i in range(n)]

    def nxt(self, pool):
        k = id(pool)
        i = self.rr.get(k, 0)
        self.rr[k] = i + 1
        return pool[i % len(pool)]

    def _deps(self, eng, reads, writes):
        waits = {}

        def need(tok):
            if tok is None:
                return
            s, v = tok
            if s is self.sem[eng] and (eng == "pe" or not SAME_ENGINE_SYNC):
                return
            k = id(s)
            if self.seen[eng].get(k, 0) < v:
                self.seen[eng][k] = v
                waits[k] = (s, v)

        for b in reads:
            need(b.w)
            if b.name.startswith("ps"):
                for t in b.r.values():
                    if t[0] is not self.sem[eng]:
                        need(t)
        for b in writes:
            need(b.w)
            for t in b.r.values():
                need(t)
        return list(waits.values())

    def op(self, eng, fn, reads=(), writes=(), inc=True):
        reads = [x.b if isinstance(x, Tile) else x for x in reads]
        writes = [x.b if isinstance(x, Tile) else x for x in writes]
        waits = self._deps(eng, reads, writes)
        tok = None
        if not inc:
            self.pending.setdefault(eng, []).extend(reads)
        if inc:
            reads = reads + self.pending.pop(eng, [])
            self.cnt[eng] += 1
            tok = (self.sem[eng], self.cnt[eng])
            for b in writes:
                b.w = tok
                b.r = {}
            for b in reads:
                b.r[id(tok[0])] = tok
        self.seq += 1
        self.ops[eng].append((waits, fn, tok, 1, self.seq))

    def dma(self, q, out_ap, in_ap, reads=(), writes=()):
        reads = [x.b if isinstance(x, Tile) else x for x in reads]
        writes = [x.b if isinstance(x, Tile) else x for x in writes]
        sb = writes[0] if writes else reads[0]
        if sb.dsem is None:
            sb.dsem = self.nc.alloc_semaphore(name=f"{self.name}_d{len(self.dbufs)}")
            self.allsems.append(sb.dsem)
            self.dbufs.append(sb)
        waits = self._deps(q, reads, writes)
        sb.dcnt += 1
        tok = (sb.dsem, 16 * sb.dcnt)
        for b in writes:
            b.w = tok
            b.r = {}
        for b in reads:
            b.r[id(tok[0])] = tok
        self.seq += 1
        self.ops[q].append((waits, lambda e: e.dma_start(out=out_ap, in_=in_ap), tok, 16, self.seq))

    def mm(self, ot, out_ap, lhsT, rhs, start, stop, reads=(), inc=None):
        if inc is None:
            inc = stop
        self.op("pe", lambda e: e.matmul(out_ap, lhsT, rhs, start=start, stop=stop),
                reads=reads, writes=[ot] if (start or inc) else [], inc=inc)

    def act(self, out_ap, in_ap, func, reads, writes, bias=None, scale=1.0):
        if bias is None:
            fn = lambda e: e.activation(out=out_ap, in_=in_ap, func=func, scale=scale)
        else:
            fn = lambda e: e.activation(out=out_ap, in_=in_ap, func=func, bias=bias, scale=scale)
        self.op("act", fn, reads, writes)

    def tt(self, out_ap, in0, in1, op, reads, writes, eng="dve"):
        self.op(eng, lambda e: e.tensor_tensor(out=out_ap, in0=in0, in1=in1, op=op), reads, writes)

    def ts(self, out_ap, in0, s1, s2, op0, op1, reads, writes, eng="dve"):
        if op1 is None:
            fn = lambda e: e.tensor_scalar(out=out_ap, in0=in0, scalar1=s1, scalar2=None, op0=op0)
        else:
            fn = lambda e: e.tensor_scalar(out=out_ap, in0=in0, scalar1=s1, scalar2=s2, op0=op0, op1=op1)
        self.op(eng, fn, reads, writes)

    def stt(self, out_ap, in0, scalar, in1, op0, op1, reads, writes, eng="dve"):
        self.op(eng, lambda e: e.scalar_tensor_tensor(out=out_ap, in0=in0, scalar=scalar, in1=in1, op0=op0, op1=op1),
                reads, writes)

    def recip(self, out_ap, in_ap, reads, writes):
        self.op("dve", lambda e: e.reciprocal(out=out_ap, in_=in_ap), reads, writes)

    def rsq(self, out_ap, in_ap, bias, reads, writes):
        self.act(out_ap, in_ap, AF.Ln, reads, writes, bias=float(bias))
        self.act(out_ap, out_ap, AF.Exp, writes, writes, scale=-0.5)

    def memset(self, out_ap, val, writes, eng="pool"):
        self.op(eng, lambda e: e.memset(out_ap, val), (), writes)

    def finish(self):
        import os
        only = os.environ.get("MK_PHASES")
        if only is not None and self.name.split("_")[0] not in only.split(","):
            for b in self.bufs:
                b.w = None
                b.r = {}
                b.dsem = None
                b.dcnt = 0
            for sm_ in self.allsems:
                self.nc.release_semaphore(sm_)
            self.es.close()
            return
        lim = None
        for item in os.environ.get("MK_LIMIT", "").split(","):
            if item.startswith(self.name.split("_")[0] + ":"):
                lim = int(item.split(":")[1])
        if lim is not None:
            for e in ENGS:
                self.ops[e] = [o for o in self.ops[e] if o[4] <= lim]
            final = {}
            for e in ENGS:
                for o in self.ops[e]:
                    if o[3] == 16:
                        final[id(o[2][0])] = max(final.get(id(o[2][0]), 0), o[2][1])
            for b in self.dbufs:
                if id(b.dsem) in final:
                    self.ops["sp"].append(([(b.dsem, final[id(b.dsem)])], None, None, 0, 0))
            print("phase", self.name, "limited to", lim, "of", self.seq)
        else:
            for b in self.dbufs:
                self.ops["sp"].append(([(b.dsem, 16 * b.dcnt)], None, None, 0, 0))
        ops = self.ops

        def body(name):
            def f(e):
                for waits, fn, tok, incv, _sq in ops[name]:
                    for s, v in waits:
                        e.wait_ge(s, v)
                    if fn is not None:
                        ins = fn(e)
                        if tok is not None:
                            ins.then_inc(tok[0], incv)
            return f

        with self.nc.Block() as block:
            block.tensor(body("pe"))
            block.scalar(body("act"))
            block.vector(body("dve"))
            block.gpsimd(body("pool"))
            block.sync(body("sp"))
        for b in self.bufs:
            b.w = None
            b.r = {}
            b.dsem = None
            b.dcnt = 0
        self.nc.clear_and_free_semaphores(self.allsems)
        with self.nc.Block():
            pass
        self.es.close()


def token_tiles(L, TT=512):
    tiles = [(t0, min(TT, L - t0), False) for t0 in range(0, L, TT)]
    tiles.append((L, CTX, True))
    return tiles


def build(L, depth, dbg=()):
    Lt = L + CTX
    nc = bass.Bass("TRN2", target_bir_lowering=False)
    dt_in = lambda n, s: nc.dram_tensor(n, list(s), F32, kind="ExternalInput").ap()
    xin = dt_in("xin", [D, Lt])
    cT = dt_in("cT", [128, NK, 2])
    smalls = dt_in("smalls", [depth, 128, NS])
    ropec = dt_in("ropec", [64, L])
    ropes = dt_in("ropes", [64, L])
    rmat = dt_in("rmat", [64, 64])
    masks = dt_in("masks", [128, 2, 512])
    w_ada = dt_in("w_ada", [depth, D, 6 * D])
    w_in = dt_in("w_in", [depth, D, INC])
    w_conv_out = dt_in("w_conv_out", [depth, 512, D])
    w_q_up = dt_in("w_q_up", [depth, 512, 1536])
    w_kv_up = dt_in("w_kv_up", [depth, 256, 2048])
    w_mla_out = dt_in("w_mla_out", [depth, 1024, D])
    w_gqa_out = dt_in("w_gqa_out", [depth, 512, D])
    w_out = dt_in("w_out", [depth, D, D])
    w_ffn_in = dt_in("w_ffn_in", [depth, D, 2 * DFF])
    w_ffn_out = dt_in("w_ffn_out", [depth, DFF, D])
    outT = nc.dram_tensor("outT", [D, L], F32, kind="ExternalOutput").ap()

    def scr(n, s, dt=BF16):
        kind = "ExternalOutput" if n in dbg else "Internal"
        return nc.dram_tensor(n, list(s), dt, kind=kind).ap()

    x_s = scr("x_s", [D, Lt], F32)
    h_s = scr("h_s", [D, Lt])
    qn_s = scr("qn_s", [8, 128, Lt])
    qr_s = scr("qr_s", [8, 64, Lt])
    kn_s = scr("kn_s", [8, 128, Lt])
    kr_s = scr("kr_s", [64, Lt])
    mv_s = scr("mv_s", [Lt, 1024])
    gq_s = scr("gq_s", [8, 64, Lt])
    gk_s = scr("gk_s", [2, 64, Lt])
    gv_s = scr("gv_s", [Lt, 128])
    u_s = scr("u_s", [512, Lt])
    cv_s = scr("cv_s", [512, Lt])
    om_s = scr("om_s", [1024, Lt])
    og_s = scr("og_s", [512, Lt])
    wm_b = scr("wm_b", [depth, 16, 128, 8192])
    wo_b = scr("wo_b", [depth, 16, 128, 2048])
    wfi_b = scr("wfi_b", [depth, 88, 128, 2048])
    wfo_b = scr("wfo_b", [depth, 32, 128, 22 * 128])
    wi_b = scr("wi_b", [depth, 128, NK * 2624])
    wq_b = scr("wq_b", [depth, 128, 4 * 1536])
    wkk_b = scr("wkk_b", [depth, 128, 2 * 1024])
    wkv_b = scr("wkv_b", [depth, 128, 2 * 1024])
    mod_dbg = scr("mod_dbg", [depth, 128, 96, 2], F32) if "mod_dbg" in dbg else None

    tiles = token_tiles(L)
    glob = ExitStack()

    def gtile(name, shape, dt):
        return Tile(glob.enter_context(nc.sbuf_tensor(name, list(shape), dt)), name)

    PS = [Tile(glob.enter_context(nc.psum_tensor(f"ps{i}", [128, 512], F32)), f"ps{i}") for i in range(8)]
    sm = gtile("sm", [128, depth, NS], F32)
    mod = [gtile(f"mod{i}", [128, 96, 2], F32) for i in range(depth)]
    A1 = [gtile(f"A1_{i}", [128, NK, 2], F32) for i in range(depth)]
    A2 = [gtile(f"A2_{i}", [128, NK, 2], F32) for i in range(depth)]
    gq_sc = gtile("gq_sc", [128, depth, 4], F32)
    gkv_sc = gtile("gkv_sc", [128, depth, 2], F32)
    ones_b = gtile("ones_b", [128, 128], BF16)
    csb = gtile("csb", [128, NK, 2], BF16)
    ones_f = gtile("ones_f", [128, 128], F32)
    rm_b = gtile("rm_b", [64, 64], BF16)
    mk_b = gtile("mk_b", [128, 2, 512], BF16)
    persistent = PS + [sm, csb, gq_sc, gkv_sc, ones_b, ones_f, rm_b, mk_b] + mod + A1 + A2

    psn = [0]

    def ps():
        psn[0] += 1
        return PS[psn[0] % 8]

    pan = [0, 0]

    def psa():
        pan[0] += 1
        return PS[pan[0] % 4]

    def pss():
        pan[1] += 1
        return PS[4 + pan[1] % 4]

    def run_all(g):
        for _ in g:
            pass

    def gen_prep(ph, l, cast_engs):
        stg_pool = ph.pool(2, [128, 2048], F32, "pstg")
        cb_pool = ph.pool(2, [128, 2048], BF16, "pcb")
        ci = [0]
        pend = []

        def flush():
            while pend:
                dst_ap, src, cb = pend.pop(0)
                ph.dma("sp", dst_ap, src, reads=[cb])

        def piece(src_ap, n, in_view, out_view, dst_ap, dst_view):
            st = ph.nxt(stg_pool)
            ph.dma("sp", in_view(st[:, :n]), src_ap, writes=[st])
            flush()
            cb = ph.nxt(cb_pool)
            ci[0] += 1
            e = cast_engs[ci[0] % len(cast_engs)]
            o_ap, i_ap = out_view(cb[:, :n]), in_view(st[:, :n])
            if e == "act":
                ph.act(o_ap, i_ap, AF.Copy, [st], [cb])
            else:
                ph.op(e, lambda en, o_ap=o_ap, i_ap=i_ap: en.tensor_copy(out=o_ap, in_=i_ap), [st], [cb])
            pend.append((dst_ap, dst_view(cb[:, :n]), cb))

        kp = lambda w: w.rearrange("(k p) n -> p k n", p=128)
        kc = lambda ap: ap.rearrange("p (k c) -> p k c", c=128)
        ident = lambda ap: ap
        kfc = lambda nk: (lambda ap: ap.rearrange("p (k f c) -> p k f c", k=nk, c=128))
        fkc = lambda nk: (lambda ap: ap.rearrange("p (f k c) -> p k f c", k=nk, c=128))
        pfn = lambda nf: (lambda ap: ap.rearrange("p (f n) -> p f n", f=nf))
        yield
        ada_pend = []

        def ada_mm(cb, ch):
            pt = ps()
            for k in range(NK):
                ph.mm(pt, pt[:, 0:2], cb[:, k * 128:(k + 1) * 128], csb[:, k, :], k == 0, k == NK - 1, reads=[cb, csb])
            ph.ts(mod[l][:, ch, :], pt[:, 0:2], sm[:, l, S_BADA + ch:S_BADA + ch + 1], None, ALU.add, None, [pt, sm], [mod[l]])

        for ch in range(96):
            st = ph.nxt(stg_pool)
            ph.dma("sp", kc(st[:, :2048]), kp(w_ada[l])[:, :, ch * 128:(ch + 1) * 128], writes=[st])
            cb = ph.nxt(cb_pool)
            ci[0] += 1
            e = cast_engs[ci[0] % len(cast_engs)]
            o_ap, i_ap = cb[:, :2048], st[:, :2048]
            if e == "act":
                ph.act(o_ap, i_ap, AF.Copy, [st], [cb])
            else:
                ph.op(e, lambda en, o_ap=o_ap, i_ap=i_ap: en.tensor_copy(out=o_ap, in_=i_ap), [st], [cb])
            if ada_pend:
                ada_mm(*ada_pend.pop(0))
            ada_pend.append((cb, ch))
            yield
        while ada_pend:
            ada_mm(*ada_pend.pop(0))
        for (A, g0, j) in ((A1, S_GMIX, 1), (A2, S_GFFN, 4)):
            for col in range(2):
                ph.ts(A[l][:, :, col], mod[l][:, j * 16:(j + 1) * 16, col], 1.0, float(np.sqrt(D)), ALU.add, ALU.mult, [mod[l]], [A[l]])
                ph.tt(A[l][:, :, col], A[l][:, :, col], sm[:, l, g0:g0 + 16], ALU.mult, [A[l], sm], [A[l]])
        if mod_dbg is not None:
            ph.dma("sp", mod_dbg[l], mod[l][:], reads=[mod[l]])
        wiv = wi_b[l].rearrange("p (k n) -> p k n", n=2624)
        for c0 in range(0, 2624, 128):
            n = min(128, 2624 - c0)
            v = (lambda n: (lambda ap: ap.rearrange("p (k c) -> p k c", c=n)))(n)
            piece(kp(w_in[l])[:, :, c0:c0 + n], 16 * n, v, v, wiv[:, :, c0:c0 + n], v)
            yield
        wqv = wq_b[l].rearrange("p (k n) -> p k n", n=1536)
        v512 = lambda ap: ap.rearrange("p (k c) -> p k c", c=512)
        for c0 in range(0, 1536, 512):
            piece(kp(w_q_up[l])[:, :, c0:c0 + 512], 2048, v512, v512, wqv[:, :, c0:c0 + 512], v512)
            yield
        hd = lambda ap: ap.rearrange("p (h d) -> p h d", d=128)
        for two, dstw in ((0, wkk_b), (1, wkv_b)):
            for k in range(2):
                piece(kp(w_kv_up[l]).rearrange("p k (h two d) -> p k h two d", two=2, d=128)[:, k, :, two, :], 1024, hd, hd,
                      dstw[l][:, k * 1024:(k + 1) * 1024], ident)
                yield
        wmv = wm_b[l].rearrange("f p n -> p f n")
        for fc in range(16):
            for g in range(3):
                piece(kp(w_in[l])[:, :, 2624 + g * 2048 + fc * 128:2624 + g * 2048 + (fc + 1) * 128], 2048, kc, kc,
                      wm_b[l, fc][:, g * 2048:(g + 1) * 2048], ident)
                yield
        for f4 in range(4):
            piece(kp(w_conv_out[l])[:, :, f4 * 512:(f4 + 1) * 512], 2048, kfc(4), fkc(4), wmv[:, f4 * 4:(f4 + 1) * 4, 6144:6656], pfn(4))
            yield
            piece(kp(w_gqa_out[l])[:, :, f4 * 512:(f4 + 1) * 512], 2048, kfc(4), fkc(4), wmv[:, f4 * 4:(f4 + 1) * 4, 7680:8192], pfn(4))
            yield
        for f2 in range(8):
            piece(kp(w_mla_out[l])[:, :, f2 * 256:(f2 + 1) * 256], 2048, kfc(8), fkc(8), wmv[:, f2 * 2:(f2 + 1) * 2, 6656:7680], pfn(2))
            yield
        for fc in range(16):
            piece(kp(w_out[l])[:, :, fc * 128:(fc + 1) * 128], 2048, kc, kc, wo_b[l, fc], ident)
            yield
        for j in range(NFF):
            for two in range(2):
                piece(kp(w_ffn_in[l])[:, :, two * DFF + j * 128:two * DFF + (j + 1) * 128], 2048, kc, kc, wfi_b[l, 2 * j + two], ident)
                yield
        for half in range(2):
            for fc in range(16):
                for kk in range(2):
                    r0 = half * 2816 + kk * 1408
                    piece(kp(w_ffn_out[l][r0:r0 + 1408, :])[:, :, fc * 128:(fc + 1) * 128], 1408, kc, kc,
                          wfo_b[l, fc * 2 + half][:, kk * 1408:(kk + 1) * 1408], ident)
                    yield
        flush()

    ph = Phase(nc, "p0", persistent)
    ph.memset(ones_b[:], 1.0, [ones_b])
    ph.memset(ones_f[:], 1.0, [ones_f])
    stg_pool = ph.pool(3, [128, 8192], F32, "stg")
    cb_pool = ph.pool(3, [128, 8192], BF16, "cb")
    ci = [0]

    def cast(out_ap, in_ap, reads, writes):
        ci[0] += 1
        e = ("dve", "act", "pool")[ci[0] % 3]
        if e == "act":
            ph.act(out_ap, in_ap, AF.Copy, reads, writes)
        else:
            ph.op(e, lambda en: en.tensor_copy(out=out_ap, in_=in_ap), reads, writes)

    def stage_cast(src_ap, n_in, in_view, out_view, n_out, q="sp"):
        st = ph.nxt(stg_pool)
        ph.dma(q, in_view(st[:, :n_in]), src_ap, writes=[st])
        cb = ph.nxt(cb_pool)
        cast(out_view(cb[:, :n_out]), in_view(st[:, :n_in]), [st], [cb])
        return cb

    st = ph.nxt(stg_pool)
    ph.dma("sp", st[:64, :64], rmat, writes=[st])
    ph.op("dve", lambda en, st=st: en.tensor_copy(out=rm_b[:], in_=st[:64, :64]), [st], [rm_b])
    st = ph.nxt(stg_pool)
    ph.dma("sp", st[:, :1024].rearrange("p (a b) -> p a b", a=2), masks, writes=[st])
    ph.op("dve", lambda en, st=st: en.tensor_copy(out=mk_b[:], in_=st[:, :1024].rearrange("p (a b) -> p a b", a=2)), [st], [mk_b])
    ph.dma("sp", sm[:], smalls.rearrange("l p n -> p l n"), writes=[sm])
    cs = ph.tile([128, NK, 2], F32, "cs")
    ph.dma("sp", cs[:], cT, writes=[cs])
    ph.act(csb[:], cs[:], AF.Silu, [cs], [csb])
    for l in range(depth):
        ph.ts(gq_sc[:, l, :], sm[:, l, S_GQA:S_GQA + 4], float(np.sqrt(512.0)), None, ALU.mult, None, [sm], [gq_sc])
        ph.ts(gkv_sc[:, l, :], sm[:, l, S_GKV:S_GKV + 2], float(np.sqrt(256.0)), None, ALU.mult, None, [sm], [gkv_sc])
    run_all(gen_prep(ph, 0, ("dve", "act", "pool")))
    ph.finish()

    for l in range(depth):
        last = (l == depth - 1)
        xsrc = xin if l == 0 else x_s
        lsm = lambda c0, n=1: sm[:, l, c0:c0 + n]

        ph = Phase(nc, f"p1_{l}", persistent)
        wi = ph.tile([128, NK, 2624], BF16, "wi")
        wq = ph.tile([128, 4, 1536], BF16, "wq")
        wkk = ph.tile([128, 2, 1024], BF16, "wkk")
        wkv = ph.tile([128, 2, 1024], BF16, "wkv")
        wiv = wi_b[l].rearrange("p (k n) -> p k n", n=2624)
        for k in range(0, NK, 4):
            ph.dma("sp" if (k // 4) % 2 else "act", wi[:, k:k + 4, :], wiv[:, k:k + 4, :], writes=[wi])
        ph.dma("sp", wq[:], wq_b[l].rearrange("p (k n) -> p k n", n=1536), writes=[wq])
        ph.dma("sp", wkk[:], wkk_b[l].rearrange("p (k n) -> p k n", n=1024), writes=[wkk])
        ph.dma("act", wkv[:], wkv_b[l].rearrange("p (k n) -> p k n", n=1024), writes=[wkv])
        xc_pool = ph.pool(4, [128, 512], F32, "xc")
        sq_pool = ph.pool(2, [128, 512], BF16, "sq")
        rstd_pool = ph.pool(2, [128, 512], F32, "rstd")
        hT_pool = ph.pool(2, [128, NK, 512], BF16, "hT")
        f32_pool = ph.pool(4, [128, 512], F32, "f32")
        b16_pool = ph.pool(4, [128, 512], BF16, "b16")
        n32_pool = ph.pool(2, [128, 4, 512], F32, "n32")
        nb_pool = ph.pool(2, [128, 4, 512], BF16, "nb")
        vb_pool = ph.pool(2, [128, 1024], BF16, "vb")
        cs_pool = ph.pool(2, [64, 2, 512], F32, "cs")
        xsv = xsrc.rearrange("(k p) t -> p k t", p=128)
        hsv = h_s.rearrange("(k p) t -> p k t", p=128)

        def rms_stats(srcs, n, T, eps_n):
            pt = ps()
            for i, src in enumerate(srcs):
                ap, tl = src() if callable(src) else src
                sq = ph.nxt(sq_pool)
                ph.act(sq[:, :T], ap, AF.Square, [tl], [sq])
                ph.mm(pt, pt[:, :T], ones_b[:], sq[:, :T], i == 0, i == len(srcs) - 1, reads=[sq, ones_b], inc=True)
            r = ph.nxt(rstd_pool)
            ph.rsq(r[:, :T], pt[:, :T], eps_n, [pt], [r])
            return r

        def rope_store(pt, np_, T, t0, is_ctx, dst_ap, cst):
            xb = ph.nxt(b16_pool)
            ph.act(xb[:np_, :T], pt[:np_, :T], AF.Copy, [pt], [xb])
            if is_ctx:
                ph.dma("sp", dst_ap, xb[:np_, :T], reads=[xb])
                return
            p2 = ps()
            ph.mm(p2, p2[:64, :T], rm_b[:], xb[:64, :T], True, True, reads=[xb, rm_b])
            t1 = ph.nxt(f32_pool)
            ph.tt(t1[:64, :T], pt[:64, :T], cst[:, 0, :T], ALU.mult, [pt, cst, xb], [t1])
            t2 = ph.nxt(f32_pool)
            ph.tt(t2[:64, :T], p2[:64, :T], cst[:, 1, :T], ALU.mult, [p2, cst], [t2])
            ob = ph.nxt(b16_pool)
            ph.tt(ob[:64, :T], t1[:64, :T], t2[:64, :T], ALU.add, [t1, t2], [ob], eng="dve" if os.environ.get("MK_T2") else "pool")
            ph.dma("sp", dst_ap, ob[:64, :T], reads=[ob])

        def norm_tile(t0, T, is_ctx):
            col = 1 if is_ctx else 0
            def ldx(k, t0=t0, T=T):
                def f():
                    xc = ph.nxt(xc_pool)
                    ph.dma("sp", xc[:, :T], xsv[:, k, t0:t0 + T], writes=[xc])
                    return xc[:, :T], xc
                return f
            rstd = rms_stats([ldx(k) for k in range(NK)], D, T, D * EPS)
            hT = ph.nxt(hT_pool)
            for k in range(NK):
                xc = ph.nxt(xc_pool)
                ph.dma("sp", xc[:, :T], xsv[:, k, t0:t0 + T], writes=[xc])
                ph.stt(xc[:, :T], xc[:, :T], A1[l][:, k, col:col + 1], rstd[:, :T], ALU.mult, ALU.mult, [xc, A1[l], rstd], [xc])
                ph.act(hT[:, k, :T], xc[:, :T], AF.Identity, [xc, mod[l]], [hT], bias=mod[l][:, k, col:col + 1])
            ph.dma("sp", hsv[:, :, t0:t0 + T], hT[:, :, :T], reads=[hT])
            if not is_ctx:
                cst = ph.nxt(cs_pool)
                ph.dma("sp", cst[:, 0, :T], ropec[:, t0:t0 + T], writes=[cst])
                ph.dma("sp", cst[:, 1, :T], ropes[:, t0:t0 + T], writes=[cst])
            else:
                cst = None
            return hT, cst

        nxt_norm = norm_tile(*tiles[0])
        for ti, (t0, T, is_ctx) in enumerate(tiles):
            col = 1 if is_ctx else 0
            skip_q = is_ctx and last
            hT, cst = nxt_norm
            if ti + 1 < len(tiles):
                nxt_norm = norm_tile(*tiles[ti + 1])

            def proj(c0, m):
                pt = ps()
                for k in range(NK):
                    ph.mm(pt, pt[:m, :T], wi[:, k, c0:c0 + m], hT[:, k, :T], k == 0, k == NK - 1, reads=[wi, hT])
                return pt

            c32 = ph.nxt(n32_pool)
            for c in range(2):
                pt = proj(c * 128, 128)
                ph.act(c32[:, c, :T], pt[:, :T], AF.Copy, [pt], [c32])
            r = rms_stats([(c32[:, c, :T], c32) for c in range(2)], 256, T, 256 * EPS)
            cn = ph.nxt(nb_pool)
            for c in range(2):
                ph.stt(cn[:, c, :T], c32[:, c, :T], gkv_sc[:, l, c:c + 1], r[:, :T], ALU.mult, ALU.mult, [c32, gkv_sc, r], [cn])
            if not skip_q:
                q32 = ph.nxt(n32_pool)
                for c in range(4):
                    pt = proj(576 + c * 128, 128)
                    ph.act(q32[:, c, :T], pt[:, :T], AF.Copy, [pt], [q32])
                r = rms_stats([(q32[:, c, :T], q32) for c in range(4)], 512, T, 512 * EPS)
                qn = ph.nxt(nb_pool)
                for c in range(4):
                    ph.stt(qn[:, c, :T], q32[:, c, :T], gq_sc[:, l, c:c + 1], r[:, :T], ALU.mult, ALU.mult, [q32, gq_sc, r], [qn])
            pt = proj(256, 64)
            rope_store(pt, 64, T, t0, is_ctx, kr_s[:, t0:t0 + T], cst)
            for g in range(2):
                pt = proj(256 if os.environ.get("MK_T1") else 320 + g * 64, 64)
                rope_store(pt, 64, T, t0, is_ctx, gk_s[g, :, t0:t0 + T], cst)
            pt = ps()
            for tc in range(T // 128):
                for k in range(NK):
                    ph.mm(pt, pt[:, tc * 128:(tc + 1) * 128], hT[:, k, tc * 128:(tc + 1) * 128], wi[:, k, 448:576], k == 0, k == NK - 1,
                          reads=[wi, hT], inc=(k == NK - 1 and tc == T // 128 - 1))
            ob = ph.nxt(b16_pool)
            ph.act(ob[:, :T], pt[:, :T], AF.Copy, [pt], [ob])
            ph.dma("sp", gv_s[t0:t0 + T, :].rearrange("(c p) d -> p c d", p=128), ob[:, :T].rearrange("p (c d) -> p c d", d=128), reads=[ob])
            if not skip_q:
                for h in range(8):
                    pt = proj(1088 + h * 64, 64)
                    rope_store(pt, 64, T, t0, is_ctx, gq_s[h, :, t0:t0 + T], cst)
                for c in range(4):
                    pb = proj(2112 + c * 128, 128)
                    sg = ph.nxt(f32_pool)
                    ph.act(sg[:, :T], pb[:, :T], AF.Sigmoid, [pb], [sg])
                    pa = proj(1600 + c * 128, 128)
                    ob = ph.nxt(b16_pool)
                    ph.tt(ob[:, :T], pa[:, :T], sg[:, :T], ALU.mult, [pa, sg], [ob])
                    ph.dma("sp", u_s[c * 128:(c + 1) * 128, t0:t0 + T], ob[:, :T], reads=[ob])
            for h in range(8):
                pt = ps()
                for k in range(2):
                    ph.mm(pt, pt[:, :T], wkk[:, k, h * 128:(h + 1) * 128], cn[:, k, :T], k == 0, k == 1, reads=[wkk, cn])
                ob = ph.nxt(b16_pool)
                ph.act(ob[:, :T], pt[:, :T], AF.Copy, [pt], [ob])
                ph.dma("sp", kn_s[h, :, t0:t0 + T], ob[:, :T], reads=[ob])
            for tc in range(T // 128):
                vb = ph.nxt(vb_pool)
                for hv in range(2):
                    pt = ps()
                    for k in range(2):
                        ph.mm(pt, pt[:, :], cn[:, k, tc * 128:(tc + 1) * 128], wkv[:, k, hv * 512:(hv + 1) * 512], k == 0, k == 1,
                              reads=[wkv, cn])
                    ph.act(vb[:, hv * 512:(hv + 1) * 512], pt[:, :], AF.Copy, [pt], [vb])
                ph.dma("sp", mv_s[t0 + tc * 128:t0 + (tc + 1) * 128, :], vb[:], reads=[vb])
            if not skip_q:
                for h in range(8):
                    pt = ps()
                    for k in range(4):
                        ph.mm(pt, pt[:, :T], wq[:, k, h * 192:h * 192 + 128], qn[:, k, :T], k == 0, k == 3, reads=[wq, qn])
                    ob = ph.nxt(b16_pool)
                    ph.act(ob[:, :T], pt[:, :T], AF.Copy, [pt], [ob])
                    ph.dma("sp", qn_s[h, :, t0:t0 + T], ob[:, :T], reads=[ob])
                    pt = ps()
                    for k in range(4):
                        ph.mm(pt, pt[:64, :T], wq[:, k, h * 192 + 128:h * 192 + 192], qn[:, k, :T], k == 0, k == 3, reads=[wq, qn])
                    rope_store(pt, 64, T, t0, is_ctx, qr_s[h, :, t0:t0 + T], cst)
        ph.finish()

        ph = Phase(nc, f"p2x_{l}", persistent)
        nkc = Lt // 128
        p_pool = ph.pool(6, [128, 512], BF16, "p")
        cvv = cv_s.rearrange("(c p) t -> p c t", p=128)

        def gen_mla():
            kr = ph.tile([64, Lt], BF16, "kr")
            ph.dma("sp", kr[:], kr_s, writes=[kr])
            kn_pool = ph.pool(2, [128, Lt], BF16, "kn")
            v_pool = ph.pool(2, [128, nkc, 128], BF16, "v")
            qn_pool = ph.pool(2, [128, Lt], BF16, "qn")
            qr_pool = ph.pool(2, [64, Lt], BF16, "qr")
            rc_pool = ph.pool(2, [128, 512], F32, "rc")
            o_pool = ph.pool(2, [128, 512], BF16, "o")
            sc_mla = float(192.0 ** -0.5)
            yield
            for h in range(8):
                kn = ph.nxt(kn_pool)
                v = ph.nxt(v_pool)
                qn = ph.nxt(qn_pool)
                qr = ph.nxt(qr_pool)
                ph.dma("sp", kn[:], kn_s[h], writes=[kn])
                ph.dma("sp", v[:], mv_s[:, h * 128:(h + 1) * 128].rearrange("(c p) d -> p c d", p=128), writes=[v])
                ph.dma("sp", qn[:], qn_s[h], writes=[qn])
                ph.dma("sp", qr[:], qr_s[h], writes=[qr])
                for (t0, T, is_ctx) in tiles:
                    if is_ctx and last:
                        continue
                    chunks = [nkc - 2, nkc - 1] if is_ctx else list(range(nkc))
                    po = psa()
                    pd = psa()
                    LA = 2
                    pend = []
                    for i, kc in enumerate(chunks + [None] * LA):
                        if kc is not None:
                            pst = pss()
                            ph.mm(pst, pst[:, :T], kn[:, kc * 128:(kc + 1) * 128], qn[:, t0:t0 + T], True, False, reads=[kn, qn], inc=False)
                            ph.mm(pst, pst[:, :T], kr[:, kc * 128:(kc + 1) * 128], qr[:, t0:t0 + T], False, True, reads=[kr, qr], inc=True)
                            p = ph.nxt(p_pool)
                            ph.act(p[:, :T], pst[:, :T], AF.Exp, [pst], [p], scale=sc_mla)
                            pend.append((i, kc, p))
                        if i >= LA:
                            j, kcj, pj = pend.pop(0)
                            fst, lst = j == 0, j == len(chunks) - 1
                            ph.mm(po, po[:, :T], v[:, kcj, :], pj[:, :T], fst, lst, reads=[v, pj], inc=lst)
                            ph.mm(pd, pd[:, :T], ones_b[:], pj[:, :T], fst, lst, reads=[pj, ones_b], inc=True)
                    rc = ph.nxt(rc_pool)
                    ph.act(rc[:, :T], pd[:, :T], AF.Ln, [pd], [rc])
                    ph.act(rc[:, :T], rc[:, :T], AF.Exp, [rc], [rc], scale=-1.0)
                    o = ph.nxt(o_pool)
                    ph.tt(o[:, :T], po[:, :T], rc[:, :T], ALU.mult, [po, rc], [o])
                    ph.dma("sp", om_s[h * 128:(h + 1) * 128, t0:t0 + T], o[:, :T], reads=[o])
                    yield

        def gen_gqa():
            gk = ph.tile([64, 2, Lt], BF16, "gk")
            gv = ph.tile([128, nkc, 128], BF16, "gv")
            ph.dma("sp", gk[:], gk_s.rearrange("g d t -> d g t"), writes=[gk])
            ph.dma("sp", gv[:], gv_s.rearrange("(c p) d -> p c d", p=128), writes=[gv])
            sinkx = ph.tile([64, 2, 512], F32, "sinkx")
            for g in range(2):
                for hh in range(4):
                    ph.act(sinkx[:, g, hh * 128:(hh + 1) * 128],
                           sm[0:64, l, S_SINK + 4 * g + hh:S_SINK + 4 * g + hh + 1].to_broadcast([64, 128]), AF.Exp, [sm], [sinkx])
            gq_pool = ph.pool(2, [64, 8, 512], BF16, "gq")
            ds_pool = ph.pool(2, [64, 512], F32, "ds")
            o_pool = ph.pool(2, [64, 512], BF16, "o")
            sc_gqa = float(64.0 ** -0.5)
            nlb = L // 128
            ogv = og_s.rearrange("(h d) t -> d h t", d=64)
            yield
            gp_pool = ph.pool(12, [128, 512], BF16, "gp")
            prevB = None

            def stageB(st):
                (t0, qb, g, plist) = st
                po = psa()
                pd = psa()
                for j, (kcj, pj) in enumerate(plist):
                    fst, lst = j == 0, j == len(plist) - 1
                    ph.mm(po, po[:64, :], gv[:, kcj, g * 64:(g + 1) * 64], pj[:], fst, lst, reads=[gv, pj], inc=lst)
                    ph.mm(pd, pd[:64, :], ones_b[:, 0:64], pj[:], fst, lst, reads=[pj, ones_b], inc=True)
                ds = ph.nxt(ds_pool)
                ph.tt(ds[:], pd[:64, :], sinkx[:, g, :], ALU.add, [pd, sinkx], [ds])
                ph.act(ds[:], ds[:], AF.Ln, [ds], [ds])
                ph.act(ds[:], ds[:], AF.Exp, [ds], [ds], scale=-1.0)
                o = ph.nxt(o_pool)
                ph.tt(o[:], po[:64, :], ds[:], ALU.mult, [po, ds], [o])
                ph.dma("sp", ogv[:, 4 * g:4 * g + 4, t0 + qb * 128:t0 + (qb + 1) * 128], o[:].rearrange("d (h t) -> d h t", t=128), reads=[o])

            for (t0, T, is_ctx) in tiles:
                if is_ctx and last:
                    continue
                gq = ph.nxt(gq_pool)
                ph.dma("sp", gq[:, :, :T], gq_s[:, :, t0:t0 + T].rearrange("h d t -> d h t"), writes=[gq])
                for qb in range(T // 128):
                    blk = (t0 + qb * 128) // 128
                    if is_ctx:
                        chunks = [(nkc - 2, None), (nkc - 1, None)]
                    else:
                        chunks = []
                        if blk > 0:
                            chunks.append((blk - 1, 0))
                        chunks.append((blk, None))
                        if blk < nlb - 1:
                            chunks.append((blk + 1, 1))
                        chunks += [(nkc - 2, None), (nkc - 1, None)]
                    for g in range(2):
                        if prevB is not None:
                            stageB(prevB)
                        plist = []
                        for (kc, mi) in chunks:
                            pst = pss()
                            ph.mm(pst, pst[:, :].rearrange("p (h t) -> p h t", t=128), gk[:, g, kc * 128:(kc + 1) * 128], gq[:, 4 * g:4 * g + 4, qb * 128:(qb + 1) * 128], True, True,
                                  reads=[gk, gq])
                            p = ph.nxt(gp_pool)
                            ph.act(p[:], pst[:], AF.Exp, [pst], [p], scale=sc_gqa)
                            if mi is not None:
                                ph.tt(p[:], p[:], mk_b[:, mi, :], ALU.mult, [p, mk_b], [p], eng="pool")
                            plist.append((kc, p))
                        prevB = (t0, qb, g, plist)
                        yield
            if prevB is not None:
                stageB(prevB)
            yield

        def gen_conv():
            u_pool = ph.pool(2, [128, 4, 512 + 30], BF16, "u")
            acc_pool = [ph.pool(2, [128, 512], F32, f"acc{c}") for c in range(4)]
            sq_pool = ph.pool(2, [128, 512], F32, "sq")
            st_pool = ph.pool(1, [128, 3, 512], F32, "st")
            cvo_pool = ph.pool(2, [128, 4, 512], BF16, "cvo")
            usv = u_s.rearrange("(c p) t -> p c t", p=128)
            cvv = cv_s.rearrange("(c p) t -> p c t", p=128)
            yield
            for (t0, T, is_ctx) in tiles:
                if is_ctx and last:
                    continue
                s0, s1 = (L, Lt) if is_ctx else (0, L)
                u = ph.nxt(u_pool)
                lo, hi = max(s0, t0 - 15), min(s1, t0 + T + 15)
                if lo > t0 - 15:
                    ph.memset(u[:, :, 0:15], 0.0, [u], eng="dve")
                if hi < t0 + T + 15:
                    ph.memset(u[:, :, T + 15:T + 30], 0.0, [u], eng="dve")
                ph.dma("sp", u[:, :, lo - (t0 - 15):hi - (t0 - 15)], usv[:, :, lo:hi], writes=[u])
                accs = [ph.nxt(acc_pool[c]) for c in range(4)]
                eng = "dve"
                for j in range(31):
                    for c in range(4):
                        acc = accs[c]
                        if j == 0:
                            ph.ts(acc[:, :T], u[:, c, 0:T], lsm(S_CW + c * 31), lsm(S_CB + c), ALU.mult, ALU.add, [u, sm], [acc], eng=eng)
                        else:
                            ph.stt(acc[:, :T], u[:, c, j:j + T], lsm(S_CW + c * 31 + j), acc[:, :T], ALU.mult, ALU.add, [u, sm, acc], [acc], eng=eng)
                    if j % 2 == 1:
                        yield
                yield
                p1 = ps()
                p2 = ps()
                for c in range(4):
                    ph.mm(p1, p1[:, :T], ones_f[:], accs[c][:, :T], c == 0, c == 3, reads=[accs[c], ones_f], inc=True)
                for c in range(4):
                    sq = ph.nxt(sq_pool)
                    ph.act(sq[:, :T], accs[c][:, :T], AF.Square, [accs[c]], [sq])
                    ph.mm(p2, p2[:, :T], ones_f[:], sq[:, :T], c == 0, c == 3, reads=[sq, ones_f], inc=True)
                st = ph.nxt(st_pool)
                ph.ts(st[:, 0, :T], p1[:, :T], 1.0 / 512, None, ALU.mult, None, [p1], [st])
                ph.tt(st[:, 1, :T], st[:, 0, :T], st[:, 0, :T], ALU.mult, [st], [st])
                ph.stt(st[:, 2, :T], p2[:, :T], 1.0 / 512, st[:, 1, :T], ALU.mult, ALU.subtract, [p2, st], [st])
                ph.rsq(st[:, 2, :T], st[:, 2, :T], EPS, [st], [st])
                cvo = ph.nxt(cvo_pool)
                for c in range(4):
                    ph.tt(accs[c][:, :T], accs[c][:, :T], st[:, 0, :T], ALU.subtract, [accs[c], st], [accs[c]])
                for c in range(4):
                    ph.tt(accs[c][:, :T], accs[c][:, :T], st[:, 2, :T], ALU.mult, [accs[c], st], [accs[c]])
                for c in range(4):
                    ph.act(cvo[:, c, :T], accs[c][:, :T], AF.Silu, [accs[c], sm], [cvo], bias=lsm(S_LNB + c), scale=lsm(S_LNG + c))
                ph.dma("sp", cvv[:, :, t0:t0 + T], cvo[:, :, :T], reads=[cvo])
                yield

        gens = [gen_mla(), gen_gqa(), gen_conv()]
        for g_ in gens:
            next(g_)
        alive = [True, True, True]
        step = 0
        while any(alive):
            step += 1
            order = [0, 1, 2, 2]
            if not alive[0]:
                order = [1, 2, 2, 2]
            for gi in order:
                if alive[gi]:
                    try:
                        next(gens[gi])
                    except StopIteration:
                        alive[gi] = False
        ph.finish()

        ph = Phase(nc, f"p2c_{l}", persistent)
        xt_pool = ph.pool(1, [128, NK, 512], F32, "xt")
        ht_pool = ph.pool(1, [128, NK, 512], BF16, "ht")
        yT_pool = ph.pool(1, [128, NK, 512], BF16, "yT")
        at_pool = ph.pool(1, [128, 22, 512], BF16, "at")
        cv_pool = ph.pool(1, [128, 4, 512], BF16, "cv")
        om_pool = ph.pool(1, [128, 8, 512], BF16, "om")
        og_pool = ph.pool(1, [128, 4, 512], BF16, "og")
        wb_pool = ph.pool(3, [128, 8192], BF16, "wb")
        sg_pool = ph.pool(3, [128, 512], F32, "sg")
        t_pool = ph.pool(3, [128, 512], F32, "tt")
        acc_pool = ph.pool(2, [128, 512], F32, "acc")
        sq_pool = ph.pool(3, [128, 512], BF16, "sq")
        rstd_pool = ph.pool(1, [128, 512], F32, "rstd")
        xdv = x_s.rearrange("(k p) t -> p k t", p=128)
        omv = om_s.rearrange("(k p) t -> p k t", p=128)
        ogv2 = og_s.rearrange("(k p) t -> p k t", p=128)
        wq_i = [0]

        prep = None
        if not last:
            prep = gen_prep(ph, l + 1, ("pool",))
            next(prep)

        def prep_step(nsteps=1):
            nonlocal prep
            for _ in range(nsteps):
                if prep is not None:
                    try:
                        next(prep)
                    except StopIteration:
                        prep = None

        def wload(src_ap, n):
            wb = ph.nxt(wb_pool)
            wq_i[0] += 1
            ph.dma("sp", wb[:, :n], src_ap, writes=[wb])
            if wq_i[0] % 3 != 0:
                prep_step()
            return wb

        xt = ph.nxt(xt_pool)
        ht = ph.nxt(ht_pool)
        yT = ph.nxt(yT_pool)
        at = ph.nxt(at_pool)
        cv = ph.nxt(cv_pool)
        om = ph.nxt(om_pool)
        og = ph.nxt(og_pool)
        c_tiles = [t for t in tiles if not (t[2] and last)]

        def load_small(t0, T):
            ph.dma("sp", ht[:, :, :T], hsv[:, :, t0:t0 + T], writes=[ht])
            ph.dma("sp", cv[:, :, :T], cvv[:, :, t0:t0 + T], writes=[cv])
            ph.dma("sp", om[:, :, :T], omv[:, :, t0:t0 + T], writes=[om])
            ph.dma("sp", og[:, :, :T], ogv2[:, :, t0:t0 + T], writes=[og])

        def load_x(t0, T):
            for k in range(0, NK, 4):
                ph.dma("sp", xt[:, k:k + 4, :T], xsv[:, k:k + 4, t0:t0 + T], writes=[xt])

        load_small(*c_tiles[0][:2])
        load_x(*c_tiles[0][:2])
        pre_w = []
        for ti, (t0, T, is_ctx) in enumerate(c_tiles):
            col = 1 if is_ctx else 0
            for fc in range(16):
                wb = pre_w.pop(0) if pre_w else wload(wm_b[l, fc], 8192)
                acc = ph.nxt(acc_pool)
                for g, (src, nk, off) in enumerate(((cv, 4, 6144), (om, 8, 6656), (og, 4, 7680))):
                    pg = ps()
                    for k in range(NK):
                        ph.mm(pg, pg[:, :T], wb[:, g * 2048 + k * 128:g * 2048 + (k + 1) * 128], ht[:, k, :T], k == 0, k == NK - 1, reads=[wb, ht])
                    sg = ph.nxt(sg_pool)
                    ph.act(sg[:, :T], pg[:, :T], AF.Sigmoid, [pg], [sg])
                    py = ps()
                    for k in range(nk):
                        ph.mm(py, py[:, :T], wb[:, off + k * 128:off + (k + 1) * 128], src[:, k, :T], k == 0, k == nk - 1, reads=[wb, src])
                    if g == 0:
                        ph.tt(acc[:, :T], py[:, :T], sg[:, :T], ALU.mult, [py, sg], [acc])
                    else:
                        tq = ph.nxt(t_pool)
                        ph.tt(tq[:, :T], py[:, :T], sg[:, :T], ALU.mult, [py, sg], [tq])
                        if g == 1:
                            ph.tt(acc[:, :T], acc[:, :T], tq[:, :T], ALU.add, [acc, tq], [acc], eng="pool")
                        else:
                            ph.tt(yT[:, fc, :T], acc[:, :T], tq[:, :T], ALU.add, [acc, tq], [yT], eng="pool")
            for f4 in range(4):
                wb = wload(wo_b[l, f4 * 4:(f4 + 1) * 4].rearrange("f p n -> p f n"), 8192)
                for fi in range(4):
                    fc = f4 * 4 + fi
                    po = ps()
                    for k in range(NK):
                        ph.mm(po, po[:, :T], wb[:, fi * 2048 + k * 128:fi * 2048 + (k + 1) * 128], yT[:, k, :T], k == 0, k == NK - 1, reads=[wb, yT])
                    ph.stt(xt[:, fc, :T], po[:, :T], mod[l][:, 2 * 16 + fc, col:col + 1], xt[:, fc, :T], ALU.mult, ALU.add, [po, mod[l], xt], [xt])
            pt = ps()
            for k in range(NK):
                sq = ph.nxt(sq_pool)
                ph.act(sq[:, :T], xt[:, k, :T], AF.Square, [xt], [sq])
                ph.mm(pt, pt[:, :T], ones_b[:], sq[:, :T], k == 0, k == NK - 1, reads=[sq, ones_b], inc=True)
            rstd = ph.nxt(rstd_pool)
            ph.rsq(rstd[:, :T], pt[:, :T], D * EPS, [pt], [rstd])
            for k in range(NK):
                tq = ph.nxt(t_pool)
                ph.stt(tq[:, :T], xt[:, k, :T], A2[l][:, k, col:col + 1], rstd[:, :T], ALU.mult, ALU.mult, [xt, A2[l], rstd], [tq])
                ph.act(ht[:, k, :T], tq[:, :T], AF.Identity, [tq, mod[l]], [ht], bias=mod[l][:, 3 * 16 + k, col:col + 1])
            for half in range(2):
                for j2 in range(0, 22, 2):
                    j = half * 22 + j2
                    wb = wload(wfi_b[l, 2 * j:2 * j + 4].rearrange("f p n -> p f n"), 8192)
                    for jj in range(2):
                        pa = ps()
                        for k in range(NK):
                            ph.mm(pa, pa[:, :T], wb[:, (2 * jj) * 2048 + k * 128:(2 * jj) * 2048 + (k + 1) * 128], ht[:, k, :T], k == 0, k == NK - 1, reads=[wb, ht])
                        sg = ph.nxt(sg_pool)
                        ph.act(sg[:, :T], pa[:, :T], AF.Silu, [pa], [sg])
                        pb = ps()
                        for k in range(NK):
                            ph.mm(pb, pb[:, :T], wb[:, (2 * jj + 1) * 2048 + k * 128:(2 * jj + 1) * 2048 + (k + 1) * 128], ht[:, k, :T], k == 0, k == NK - 1, reads=[wb, ht])
                        ph.tt(at[:, j2 + jj, :T], pb[:, :T], sg[:, :T], ALU.mult, [pb, sg], [at])
                for fc in range(16):
                    wb = wload(wfo_b[l, fc * 2 + half], 22 * 128)
                    po = ps()
                    for k in range(22):
                        ph.mm(po, po[:, :T], wb[:, k * 128:(k + 1) * 128], at[:, k, :T], k == 0, k == 21, reads=[wb, at])
                    ph.stt(xt[:, fc, :T], po[:, :T], mod[l][:, 5 * 16 + fc, col:col + 1], xt[:, fc, :T], ALU.mult, ALU.add, [po, mod[l], xt], [xt])
            if ti + 1 < len(c_tiles):
                load_small(*c_tiles[ti + 1][:2])
                pre_w = [wload(wm_b[l, 0], 8192), wload(wm_b[l, 1], 8192)]
            for k in range(0, NK, 4):
                ph.dma("sp", xdv[:, k:k + 4, t0:t0 + T], xt[:, k:k + 4, :T], reads=[xt])
            if ti + 1 < len(c_tiles):
                load_x(*c_tiles[ti + 1][:2])
        while prep is not None:
            prep_step()
        ph.finish()

    ph = Phase(nc, "fin", persistent)
    xt_pool = ph.pool(2, [128, NK, 512], F32, "xt")
    sq_pool = ph.pool(3, [128, 512], BF16, "sq")
    rstd_pool = ph.pool(2, [128, 512], F32, "rstd")
    gf = ph.tile([128, NK], F32, "gf")
    ph.ts(gf[:], sm[:, 0, S_GFIN:S_GFIN + 16], float(np.sqrt(D)), None, ALU.mult, None, [sm], [gf])
    xdv = x_s.rearrange("(k p) t -> p k t", p=128)
    odv = outT.rearrange("(k p) t -> p k t", p=128)
    for (t0, T, is_ctx) in tiles:
        if is_ctx:
            continue
        xt = ph.nxt(xt_pool)
        for k in range(0, NK, 4):
            ph.dma("sp", xt[:, k:k + 4, :T], xdv[:, k:k + 4, t0:t0 + T], writes=[xt])
        pt = ps()
        for k in range(NK):
            sq = ph.nxt(sq_pool)
            ph.act(sq[:, :T], xt[:, k, :T], AF.Square, [xt], [sq])
            ph.mm(pt, pt[:, :T], ones_b[:], sq[:, :T], k == 0, k == NK - 1, reads=[sq, ones_b], inc=True)
        rstd = ph.nxt(rstd_pool)
        ph.rsq(rstd[:, :T], pt[:, :T], D * EPS, [pt], [rstd])
        for k in range(NK):
            ph.stt(xt[:, k, :T], xt[:, k, :T], gf[:, k:k + 1], rstd[:, :T], ALU.mult, ALU.mult, [xt, gf, rstd], [xt])
        for k in range(0, NK, 4):
            ph.dma("sp", odv[:, k:k + 4, t0:t0 + T], xt[:, k:k + 4, :T], reads=[xt])
    ph.finish()
    glob.close()
    return nc


def host_consts(L):
    GRID_W = 64
    rows = L // GRID_W
    row = np.repeat(np.arange(rows), GRID_W).astype(np.float32)
    colp = np.tile(np.arange(GRID_W), rows).astype(np.float32)
    inv_freq = (np.float32(10000.0) ** (-np.arange(16, dtype=np.float32) / np.float32(16))).astype(np.float32)
    a_row = row[:, None] * inv_freq[None, :]
    a_col = colp[:, None] * inv_freq[None, :]
    ang = np.concatenate([a_row, a_row, a_col, a_col], axis=-1).astype(np.float32)
    ropec = np.ascontiguousarray(np.cos(ang).T.astype(np.float32))
    ropes = np.ascontiguousarray(np.sin(ang).T.astype(np.float32))
    rmat = np.zeros((64, 64), np.float32)
    for m in range(64):
        if m % 32 < 16:
            rmat[m + 16, m] = -1.0
        else:
            rmat[m - 16, m] = 1.0
    j = np.arange(128)[:, None]
    i = np.arange(128)[None, :]
    mA = (j >= i).astype(np.float32)
    mB = (j <= i).astype(np.float32)
    masks = np.stack([np.tile(mA, (1, 4)), np.tile(mB, (1, 4))], axis=1)
    return ropec, ropes, rmat, np.ascontiguousarray(masks)


def pack_smalls(depth, b_ada, g_mix, g_ffn, g_q_a, g_kv_a, conv_b, conv_ln_g, conv_ln_b, conv_w, gqa_sink, g_final):
    sm = np.zeros((depth, 128, NS), np.float32)

    def fm(v):
        return np.asarray(v, np.float32).reshape(-1, 128).T

    for l in range(depth):
        sm[l, :, S_GMIX:S_GMIX + 16] = fm(g_mix[l])
        sm[l, :, S_GFFN:S_GFFN + 16] = fm(g_ffn[l])
        sm[l, :, S_BADA:S_BADA + 96] = fm(b_ada[l])
        sm[l, :, S_GQA:S_GQA + 4] = fm(g_q_a[l])
        sm[l, :, S_GKV:S_GKV + 2] = fm(g_kv_a[l])
        sm[l, :, S_CB:S_CB + 4] = fm(conv_b[l])
        sm[l, :, S_LNG:S_LNG + 4] = fm(conv_ln_g[l])
        sm[l, :, S_LNB:S_LNB + 4] = fm(conv_ln_b[l])
        cw = np.asarray(conv_w[l], np.float32)
        for c in range(4):
            sm[l, :, S_CW + c * 31:S_CW + (c + 1) * 31] = cw[:, c * 128:(c + 1) * 128].T
        sm[l, :, S_SINK:S_SINK + 8] = np.asarray(gqa_sink[l], np.float32)[None, :]
        sm[l, :, S_GFIN:S_GFIN + 16] = fm(g_final)
    return sm


_CACHE = {}


def run(inputs, L, depth, n_cores, dbg=()):
    key = (L, depth, tuple(dbg))
    if key not in _CACHE:
        _CACHE[key] = build(L, depth, dbg)
    nc = _CACHE[key]
    f = lambda a: np.ascontiguousarray(np.asarray(a, np.float32))
    x, c, ctx, c_ctx = f(inputs["x"]), f(inputs["c"]), f(inputs["ctx"]), f(inputs["c_ctx"])
    ropec, ropes, rmat, masks = host_consts(L)
    sm = pack_smalls(depth, *[np.asarray(inputs[k]) for k in ("b_ada", "g_mix", "g_ffn", "g_q_a", "g_kv_a", "conv_b", "conv_ln_g",
                                                               "conv_ln_b", "conv_w", "gqa_sink")], np.asarray(inputs["g_final"]))
    shared = {"smalls": sm, "ropec": ropec, "ropes": ropes, "rmat": rmat, "masks": masks}
    for k in ("w_ada", "w_in", "w_conv_out", "w_q_up", "w_kv_up", "w_mla_out", "w_gqa_out", "w_out", "w_ffn_in", "w_ffn_out"):
        shared[k] = f(inputs[k])[:depth]
    in_maps = []
    for b in range(n_cores):
        m = dict(shared)
        m["xin"] = np.ascontiguousarray(np.concatenate([x[b].T, ctx[b].T], axis=1))
        cc = np.stack([c[b], c_ctx], axis=1)
        m["cT"] = np.ascontiguousarray(cc.reshape(NK, 128, 2).transpose(1, 0, 2))
        in_maps.append(m)
    res = run_bass_kernel_spmd(nc, in_maps, core_ids=list(range(n_cores)))
    return res


def kernel(**inputs):
    B, L, _ = inputs["x"].shape
    res = run(inputs, L, 4, B)
    out = np.stack([np.ascontiguousarray(r["outT"].T) for r in res.results], axis=0)
    return out.astype(np.float32)
```
